# Optimizing a Trainium2 kernel written in Bass

```python
import math
import jax
import jax.numpy as jnp
from jax import lax
import numpy as np

D_MODEL = 2048
BATCH = 4
SEQ = 2048
DEPTH = 2

GRID_W = 64
CTX_LEN = 256
EPS = 1e-6
ROPE_THETA = 10000.0
Q_BLOCK = 128
MLA_HEADS = 4
MLA_Q_RANK = 512
MLA_KV_RANK = 512
MLA_NOPE = 128
MLA_ROPE = 64
MLA_V = 128
GQA_HEADS = 8
GQA_KV_HEADS = 2
GQA_HD = 128
GDN_HEADS = 4
GDN_DK = 128
GDN_DV = 128
GDN_CONV = 5
GDN_CHUNK = 64
N_BRANCH = 3
D_FF = 4 * D_MODEL
N_MOD = 6
IN_SPLITS = (MLA_Q_RANK, MLA_KV_RANK, MLA_ROPE,
             GQA_HEADS * GQA_HD, GQA_KV_HEADS * GQA_HD, GQA_KV_HEADS * GQA_HD,
             GDN_HEADS * (2 * GDN_DK + GDN_DV), 2 * GDN_HEADS, 2 * GDN_HEADS, GDN_HEADS * GDN_DV,
             N_BRANCH * D_MODEL)
D_IN = sum(IN_SPLITS)

kernel_name = 'hybrid_mla_gqa_gdn_prefix_dit_block'


def rmsnorm(x, g):
    xf = x.astype(jnp.float32)
    y = xf * lax.rsqrt(jnp.mean(xf * xf, axis=-1, keepdims=True) + EPS)
    return (y * g.astype(jnp.float32)).astype(x.dtype)


def l2norm(x):
    xf = x.astype(jnp.float32)
    return (xf * lax.rsqrt(jnp.sum(xf * xf, axis=-1, keepdims=True) + EPS)).astype(x.dtype)


def split_cols(p):
    out, off = [], 0
    for n in IN_SPLITS:
        out.append(p[..., off:off + n])
        off += n
    return out


def rope_1d(x, pos):
    half = x.shape[-1] // 2
    inv_freq = ROPE_THETA ** (-jnp.arange(half, dtype=jnp.float32) / half)
    ang = pos.astype(jnp.float32)[:, None] * inv_freq[None, :]
    cos = jnp.cos(ang)[:, None, :]
    sin = jnp.sin(ang)[:, None, :]
    xf = x.astype(jnp.float32)
    x1, x2 = xf[..., :half], xf[..., half:]
    return jnp.concatenate([x1 * cos - x2 * sin, x2 * cos + x1 * sin], axis=-1).astype(x.dtype)


def rope_2d(x, row, col):
    n = x.shape[-1] // 2
    return jnp.concatenate([rope_1d(x[..., :n], row), rope_1d(x[..., n:], col)], axis=-1)


def blocked_attention(q, k, v, scale):
    b, t, hq, d = q.shape
    hk = k.shape[2]
    grp = hq // hk
    nb = t // Q_BLOCK
    qb = q.reshape(b, nb, Q_BLOCK, hk, grp, d).transpose(1, 0, 2, 3, 4, 5)

    def one_block(qblk):
        s = jnp.einsum('bqhgd,bshd->bhgqs', qblk, k).astype(jnp.float32) * scale
        p = jax.nn.softmax(s, axis=-1).astype(v.dtype)
        return jnp.einsum('bhgqs,bshe->bqhge', p, v)

    o = lax.map(one_block, qb)
    return o.transpose(1, 0, 2, 3, 4, 5).reshape(b, t, hq, v.shape[-1])


def centred_dwconv(x, w):
    pad = GDN_CONV // 2
    return lax.conv_general_dilated(x, w[:, None, :].astype(x.dtype), window_strides=(1,),
                                    padding=((pad, pad),), dimension_numbers=('NWC', 'WIO', 'NWC'),
                                    feature_group_count=x.shape[-1])


def gated_delta_chunked(q, k, v, g, beta, s0):
    f32 = jnp.float32
    b, t, h, dk = q.shape
    cs = GDN_CHUNK
    n = t // cs

    def chunks(a):
        a = a.astype(f32).reshape((b, n, cs, h) + a.shape[3:])
        return jnp.moveaxis(a, (1, 3), (0, 2))

    qc = chunks(q) * (dk ** -0.5)
    kc, vc, bc = chunks(k), chunks(v), chunks(beta)
    gc = jnp.cumsum(chunks(g), axis=-1)
    lower = jnp.tril(jnp.ones((cs, cs), bool))
    strict = jnp.tril(jnp.ones((cs, cs), bool), -1)
    diff = gc[..., :, None] - gc[..., None, :]
    decay = jnp.where(lower, jnp.exp(jnp.where(lower, diff, 0.0)), 0.0)
    kb = kc * bc[..., None]
    a = jnp.where(strict, jnp.einsum('nbhid,nbhjd->nbhij', kb, kc) * decay, 0.0)
    eye = jnp.broadcast_to(jnp.eye(cs, dtype=f32), a.shape)
    tinv = lax.linalg.triangular_solve(eye + a, eye, left_side=True, lower=True, unit_diagonal=True)
    u = jnp.einsum('nbhij,nbhje->nbhie', tinv, vc * bc[..., None])
    w = jnp.einsum('nbhij,nbhjd->nbhid', tinv, kb * jnp.exp(gc)[..., None])
    qk = jnp.einsum('nbhid,nbhjd->nbhij', qc, kc) * decay

    def step(s, inp):
        q_i, k_i, u_i, w_i, g_i, qk_i = inp
        v_new = u_i - jnp.einsum('bhcd,bhde->bhce', w_i, s)
        o = (jnp.einsum('bhcd,bhde->bhce', q_i * jnp.exp(g_i)[..., None], s)
             + jnp.einsum('bhij,bhje->bhie', qk_i, v_new))
        g_last = g_i[..., -1:]
        s = (s * jnp.exp(g_last)[..., None]
             + jnp.einsum('bhcd,bhce->bhde', k_i * jnp.exp(g_last - g_i)[..., None], v_new))
        return s, o

    s_fin, o = lax.scan(step, s0.astype(f32), (qc, kc, u, w, gc, qk))
    o = jnp.moveaxis(o, (0, 2), (1, 3)).reshape(b, t, h, -1)
    return o, s_fin


def gdn_mixer(qkv_raw, beta_raw, a_raw, w_conv, a_log, dt_bias, s0_f, s0_b):
    b, t, _ = qkv_raw.shape
    qkv = jax.nn.silu(centred_dwconv(qkv_raw, w_conv))
    q, k, v = jnp.split(qkv, [GDN_HEADS * GDN_DK, 2 * GDN_HEADS * GDN_DK], axis=-1)
    q = l2norm(q.reshape(b, t, GDN_HEADS, GDN_DK))
    k = l2norm(k.reshape(b, t, GDN_HEADS, GDN_DK))
    v = v.reshape(b, t, GDN_HEADS, GDN_DV)
    beta = jax.nn.sigmoid(beta_raw.astype(jnp.float32)).reshape(b, t, 2, GDN_HEADS)
    g = -jnp.exp(a_log.astype(jnp.float32)) * jax.nn.softplus(
        a_raw.astype(jnp.float32).reshape(b, t, 2, GDN_HEADS) + dt_bias.astype(jnp.float32))
    o_f, s_f = gated_delta_chunked(q, k, v, g[:, :, 0], beta[:, :, 0], s0_f)
    rev = lambda arr: jnp.flip(arr, axis=1)
    o_b, s_b = gated_delta_chunked(rev(q), rev(k), rev(v), rev(g[:, :, 1]), rev(beta[:, :, 1]), s0_b)
    return o_f + rev(o_b), s_f, s_b


def gdn_output(o, gate_raw, g_out, dtype):
    b, t = o.shape[:2]
    gate = jax.nn.silu(gate_raw.astype(jnp.float32).reshape(b, t, GDN_HEADS, GDN_DV))
    return (rmsnorm(o, g_out) * gate).reshape(b, t, GDN_HEADS * GDN_DV).astype(dtype)


def project_mixers(h, w_in, g_mla_q, w_mla_qb, g_mla_kv, w_mla_kvb, g_gqa_q, g_gqa_k, pos):
    b, t, _ = h.shape
    (cq, ckv, kpe, gq, gk, gv, qkv, beta, dec, ogate, bgate) = split_cols(h @ w_in)
    qa = (rmsnorm(cq, g_mla_q) @ w_mla_qb).reshape(b, t, MLA_HEADS, MLA_NOPE + MLA_ROPE)
    kva = (rmsnorm(ckv, g_mla_kv) @ w_mla_kvb).reshape(b, t, MLA_HEADS, MLA_NOPE + MLA_V)
    q_nope, q_pe = qa[..., :MLA_NOPE], qa[..., MLA_NOPE:]
    k_nope, va = kva[..., :MLA_NOPE], kva[..., MLA_NOPE:]
    k_pe = kpe.reshape(b, t, 1, MLA_ROPE)
    qb = rmsnorm(gq.reshape(b, t, GQA_HEADS, GQA_HD), g_gqa_q)
    kb = rmsnorm(gk.reshape(b, t, GQA_KV_HEADS, GQA_HD), g_gqa_k)
    vb = gv.reshape(b, t, GQA_KV_HEADS, GQA_HD)
    if pos is not None:
        row, col = pos
        q_pe, k_pe = rope_2d(q_pe, row, col), rope_2d(k_pe, row, col)
        qb, kb = rope_2d(qb, row, col), rope_2d(kb, row, col)
    qa = jnp.concatenate([q_nope, q_pe], axis=-1)
    ka = jnp.concatenate([k_nope, jnp.broadcast_to(k_pe, (b, t, MLA_HEADS, MLA_ROPE))], axis=-1)
    return (qa, ka, va, qb, kb, vb, qkv, beta, dec, ogate, bgate)


def merge_branches(o_a, o_b, o_c, bgate, w_up_a, w_up_b, w_up_c, w_out):
    ga, gb, gc = jnp.split(jax.nn.sigmoid(bgate), N_BRANCH, axis=-1)
    y = ga * (o_a @ w_up_a) + gb * (o_b @ w_up_b) + gc * (o_c @ w_up_c)
    return y @ w_out


def sqrelu_mlp(h, w1, w2):
    return jnp.square(jax.nn.relu(h @ w1)) @ w2


def setup_inputs(seed: int = 0) -> dict:
    key = jax.random.key(seed)
    ks = jax.random.split(key, 26)
    f32 = jnp.float32
    L, D = DEPTH, D_MODEL

    def dense(k, fan_in, shape, gain=1.0):
        return jax.random.normal(k, shape, f32) * (gain * fan_in ** -0.5)

    def norm_gain(k, shape):
        return 1.0 + 0.05 * jax.random.normal(k, shape, f32)

    dt = jnp.exp(jax.random.uniform(ks[16], (L, 2, GDN_HEADS), f32, math.log(1e-3), math.log(1e-1)))
    return {
        'x': jax.random.normal(ks[0], (BATCH, SEQ, D), f32),
        'c': jax.random.normal(ks[1], (BATCH, D), f32),
        'ctx': jax.random.normal(ks[2], (BATCH, CTX_LEN, D), f32),
        'c_ctx': jax.random.normal(ks[3], (D,), f32),
        'w_mod': dense(ks[4], D, (L, D, N_MOD * D), 0.5),
        'b_mod': 0.02 * jax.random.normal(ks[5], (L, N_MOD * D), f32),
        'g_norm1': norm_gain(ks[6], (L, D)),
        'w_in': dense(ks[7], D, (L, D, D_IN)),
        'g_mla_q': norm_gain(ks[8], (L, MLA_Q_RANK)),
        'w_mla_qb': dense(ks[9], MLA_Q_RANK, (L, MLA_Q_RANK, MLA_HEADS * (MLA_NOPE + MLA_ROPE))),
        'g_mla_kv': norm_gain(ks[10], (L, MLA_KV_RANK)),
        'w_mla_kvb': dense(ks[11], MLA_KV_RANK, (L, MLA_KV_RANK, MLA_HEADS * (MLA_NOPE + MLA_V))),
        'g_gqa_q': norm_gain(ks[12], (L, GQA_HD)),
        'g_gqa_k': norm_gain(ks[13], (L, GQA_HD)),
        'w_conv': dense(ks[14], GDN_CONV, (L, GDN_CONV, GDN_HEADS * (2 * GDN_DK + GDN_DV))),
        'a_log': jnp.log(jax.random.uniform(ks[15], (L, 2, GDN_HEADS), f32, 1.0, 16.0)),
        'dt_bias': dt + jnp.log(-jnp.expm1(-dt)),
        'g_gdn_out': norm_gain(ks[17], (L, GDN_DV)),
        'w_up_a': dense(ks[18], MLA_HEADS * MLA_V, (L, MLA_HEADS * MLA_V, D)),
        'w_up_b': dense(ks[19], GQA_HEADS * GQA_HD, (L, GQA_HEADS * GQA_HD, D)),
        'w_up_c': dense(ks[20], GDN_HEADS * GDN_DV, (L, GDN_HEADS * GDN_DV, D)),
        'w_out': dense(ks[21], D, (L, D, D)),
        'g_norm2': norm_gain(ks[22], (L, D)),
        'w_ff1': dense(ks[23], D, (L, D, D_FF)),
        'w_ff2': dense(ks[24], D_FF, (L, D_FF, D), 0.5),
        'g_final': norm_gain(ks[25], (D,)),
    }


def reference(x, c, ctx, c_ctx, w_mod, b_mod, g_norm1, w_in, g_mla_q, w_mla_qb, g_mla_kv, w_mla_kvb,
              g_gqa_q, g_gqa_k, w_conv, a_log, dt_bias, g_gdn_out, w_up_a, w_up_b, w_up_c, w_out,
              g_norm2, w_ff1, w_ff2, g_final):
    b, t, _ = x.shape
    rows = t // GRID_W
    row = jnp.repeat(jnp.arange(rows, dtype=jnp.int32), GRID_W)
    col = jnp.tile(jnp.arange(GRID_W, dtype=jnp.int32), rows)
    pos = (row, col)
    mla_scale = (MLA_NOPE + MLA_ROPE) ** -0.5
    gqa_scale = GQA_HD ** -0.5
    s_zero = jnp.zeros((b, GDN_HEADS, GDN_DK, GDN_DV), jnp.float32)
    sc = jax.nn.silu(c)
    scc = jax.nn.silu(c_ctx)
    for l in range(DEPTH):
        last = l == DEPTH - 1
        mod_x = jnp.split((sc @ w_mod[l] + b_mod[l])[:, None, :], N_MOD, axis=-1)
        mod_c = jnp.split(scc @ w_mod[l] + b_mod[l], N_MOD, axis=-1)
        proj_w = (w_in[l], g_mla_q[l], w_mla_qb[l], g_mla_kv[l], w_mla_kvb[l], g_gqa_q[l], g_gqa_k[l])
        hx = rmsnorm(x, g_norm1[l]) * (1 + mod_x[1]) + mod_x[0]
        hc = rmsnorm(ctx, g_norm1[l]) * (1 + mod_c[1]) + mod_c[0]
        (qa_x, ka_x, va_x, qb_x, kb_x, vb_x, qkv_x, beta_x, dec_x, og_x, bg_x) = project_mixers(hx, *proj_w, pos)
        (qa_c, ka_c, va_c, qb_c, kb_c, vb_c, qkv_c, beta_c, dec_c, og_c, bg_c) = project_mixers(hc, *proj_w, None)
        oc_c, s_f, s_b = gdn_mixer(qkv_c, beta_c, dec_c, w_conv[l], a_log[l], dt_bias[l], s_zero, s_zero)
        oc_x, _, _ = gdn_mixer(qkv_x, beta_x, dec_x, w_conv[l], a_log[l], dt_bias[l], s_f, s_b)
        oa_x = blocked_attention(qa_x, jnp.concatenate([ka_x, ka_c], axis=1),
                                 jnp.concatenate([va_x, va_c], axis=1), mla_scale)
        ob_x = blocked_attention(qb_x, jnp.concatenate([kb_x, kb_c], axis=1),
                                 jnp.concatenate([vb_x, vb_c], axis=1), gqa_scale)
        mix_x = merge_branches(oa_x.reshape(b, t, -1), ob_x.reshape(b, t, -1),
                               gdn_output(oc_x, og_x, g_gdn_out[l], x.dtype), bg_x,
                               w_up_a[l], w_up_b[l], w_up_c[l], w_out[l])
        x = x + mod_x[2] * mix_x
        hx2 = rmsnorm(x, g_norm2[l]) * (1 + mod_x[4]) + mod_x[3]
        x = x + mod_x[5] * sqrelu_mlp(hx2, w_ff1[l], w_ff2[l])
        if not last:
            tc = ctx.shape[1]
            oa_c = blocked_attention(qa_c, ka_c, va_c, mla_scale)
            ob_c = blocked_attention(qb_c, kb_c, vb_c, gqa_scale)
            mix_c = merge_branches(oa_c.reshape(b, tc, -1), ob_c.reshape(b, tc, -1),
                                   gdn_output(oc_c, og_c, g_gdn_out[l], ctx.dtype), bg_c,
                                   w_up_a[l], w_up_b[l], w_up_c[l], w_out[l])
            ctx = ctx + mod_c[2] * mix_c
            hc2 = rmsnorm(ctx, g_norm2[l]) * (1 + mod_c[4]) + mod_c[3]
            ctx = ctx + mod_c[5] * sqrelu_mlp(hc2, w_ff1[l], w_ff2[l])
    return rmsnorm(x, g_final)
```

```python
import math
from contextlib import ExitStack

import numpy as np
import ml_dtypes

import concourse.bass as bass
import concourse.mybir as mybir
from concourse.bass_utils import run_bass_kernel_spmd

F32 = mybir.dt.float32
BF16 = mybir.dt.bfloat16
AF = mybir.ActivationFunctionType
ALU = mybir.AluOpType
AX = mybir.AxisListType

D = 2048
SEQ = 2048
NCTX = 256
NOWN = 1024
NTOK = NCTX + SEQ
NQ = NCTX + NOWN
EPS = 1e-6
DFF = 8192
NKV = 1104
NQC = 2048
NG = 1536

SAME_ENG_SYNC = True


class Buf:
    __slots__ = ("name", "w", "r", "sem", "psum")

    def __init__(self, name):
        self.name = name
        self.w = None
        self.r = []
        self.sem = None
        self.psum = False


class Tile:
    def __init__(self, t, name):
        self.t = t
        self.name = name
        self.b = Buf(name)
        self.subs = {}

    def k(self, key):
        s = self.subs.get(key)
        if s is None:
            s = Buf(f"{self.name}.{key}")
            self.subs[key] = s
        return s

    def all(self):
        return [self.b] + list(self.subs.values())


class Op:
    __slots__ = ("eng", "fn", "deps", "signal", "count", "dma", "sem", "idx", "phase")

    def __init__(self, eng, fn, deps, dma=False, sem=None):
        self.eng = eng
        self.fn = fn
        self.deps = deps
        self.signal = dma
        self.count = 0
        self.dma = dma
        self.sem = sem
        self.idx = 0


class Sched:
    ENGS = ("pe", "act", "dve", "pool", "sp")

    def __init__(self, nc, stack):
        self.nc = nc
        self.gstack = stack
        self.pstack = None
        self.ops = []
        self.e = {"pe": nc.tensor, "act": nc.scalar, "dve": nc.vector,
                  "pool": nc.gpsimd, "sp": nc.sync}
        self.esem = {e: stack.enter_context(nc.semaphore(f"sem_{e}")) for e in self.ENGS}
        self.cnt = {e: 0 for e in self.ENGS}
        self.waited = {e: {} for e in self.ENGS}
        self.dcnt = {}
        self.issued = {}
        self.semobj = {}
        self.dma_sem_pools = {}
        self.n_dma_sems = 0
        self.phase_id = 0
        self.n_total = 0
        self.hooks = []

    def _stack(self, local):
        return self.pstack if (local and self.pstack is not None) else self.gstack

    def sbuf(self, name, shape, dtype, local=True):
        if local and getattr(self, "arena", None) is not None:
            tl = self.arena.alloc(name, list(shape)[1:], dtype)
            self.arena.release(tl)
            return tl
        t = self._stack(local).enter_context(self.nc.sbuf_tensor(name, list(shape), dtype))
        return Tile(t, name)

    def psum(self, name, shape, dtype=F32, local=True):
        assert list(shape) == [128, 512] and dtype == F32
        t = self._stack(local).enter_context(self.nc.psum_tensor(name, list(shape), dtype))
        tl = Tile(t, name)
        tl.b.psum = True
        return tl

    def dram(self, name, shape, dtype, kind="Internal"):
        t = self.nc.dram_tensor(name, list(shape), dtype, kind=kind)
        return Tile(t, name)

    def phase(self):
        return _Phase(self)

    def _deps(self, reads, writes, op):
        deps = {}
        for b in reads:
            if b.w is not None:
                deps[b.w] = True
            if b.psum:
                for r in b.r:
                    if r.eng != op.eng:
                        deps.setdefault(r, False)
        for b in writes:
            if b.w is not None:
                deps.setdefault(b.w, False)
            for r in b.r:
                deps.setdefault(r, False)
        deps.pop(op, None)
        return {d: raw for d, raw in deps.items() if d.phase == self.phase_id}

    def _commit(self, reads, writes, op):
        for b in writes:
            b.w = op
            b.r = []
        for b in reads:
            if op.dma:
                b.r.append(op)
            else:
                b.r = [o for o in b.r if o.dma or o.eng != op.eng]
                b.r.append(op)

    def op(self, eng, fn, reads=(), writes=()):
        o = Op(eng, fn, None)
        o.phase = self.phase_id
        o.deps = self._deps(reads, writes, o)
        self._commit(reads, writes, o)
        self.ops.append(o)
        return o

    def dma(self, q, out, in_, reads=(), writes=(), slot=None):
        assert slot is not None
        o = Op(q, (out, in_), None, dma=True)
        o.phase = self.phase_id
        o.deps = self._deps(reads, writes, o)
        o.sem = slot
        self._commit(reads, writes, o)
        self.ops.append(o)
        return o

    def emit(self):
        nc = self.nc
        ops = self.ops
        self.ops = []
        self.n_total += len(ops)
        for o in ops:
            for d, raw in o.deps.items():
                if d.dma:
                    continue
                if d.eng != o.eng:
                    d.signal = True
                elif SAME_ENG_SYNC and d.eng != "pe":
                    d.signal = True
        last = {}
        for o in ops:
            if not o.dma:
                last[o.eng] = o
        for o in last.values():
            o.signal = True
        cnt, dcnt, issued, semobj = self.cnt, self.dcnt, self.issued, self.semobj
        phase_bufs = []
        for o in ops:
            if o.dma:
                b = o.sem
                if b.sem is None:
                    pool = self.dma_sem_pools.setdefault(o.eng, [])
                    if pool:
                        b.sem = pool.pop()
                    else:
                        b.sem = self.gstack.enter_context(nc.semaphore(f"dsem_{self.n_dma_sems}"))
                        self.n_dma_sems += 1
                        dcnt[id(b.sem)] = 0
                        semobj[id(b.sem)] = b.sem
                    phase_bufs.append((b, o.eng))
                dcnt[id(b.sem)] += 16
                o.count = dcnt[id(b.sem)]
            elif o.signal:
                cnt[o.eng] += 1
                o.count = cnt[o.eng]
        for e in self.ENGS:
            semobj[id(self.esem[e])] = self.esem[e]
        for o in ops:
            need = {}
            for d, raw in o.deps.items():
                if d.dma:
                    sm = d.sem.sem
                    v = max(d.count, issued.get(id(sm), 0))
                else:
                    if d.eng == o.eng and (d.eng == "pe" or not SAME_ENG_SYNC):
                        continue
                    sm = self.esem[d.eng]
                    v = d.count
                if need.get(id(sm), 0) < v:
                    need[id(sm)] = v
            eng = self.e[o.eng]
            w = self.waited[o.eng]
            for k, v in need.items():
                if w.get(k, 0) < v:
                    eng.wait_ge(semobj[k], v)
                    w[k] = v
            if o.dma:
                out, in_ = o.fn
                ins = eng.dma_start(out=out, in_=in_)
                ins.then_inc(o.sem.sem, 16)
                issued[id(o.sem.sem)] = o.count
            else:
                ins = o.fn(eng)
                if o.signal:
                    ins.then_inc(self.esem[o.eng], 1)
            o.fn = None
        for e in self.ENGS:
            eng = self.e[e]
            w = self.waited[e]
            for e2 in self.ENGS:
                k = id(self.esem[e2])
                if e2 != e and cnt[e2] > w.get(k, 0):
                    eng.wait_ge(self.esem[e2], cnt[e2])
                    w[k] = cnt[e2]
            for k, v in issued.items():
                if w.get(k, 0) < v:
                    eng.wait_ge(semobj[k], v)
                    w[k] = v
        for b, q in phase_bufs:
            self.dma_sem_pools[q].append(b.sem)
            b.sem = None
        self.phase_id += 1
        for h in self.hooks:
            h()


class _Phase:
    def __init__(self, s):
        self.s = s

    def __enter__(self):
        self.st = ExitStack()
        self.st.__enter__()
        self.s.pstack = self.st
        return self

    def __exit__(self, *a):
        if a[0] is None:
            self.s.emit()
        self.s.pstack = None
        return self.st.__exit__(*a)


class Ctx:
    pass


def mm(s, out_ap, lhsT_ap, rhs_ap, start, stop, reads, writes):
    return s.op("pe", lambda e: e.matmul(out_ap, lhsT_ap, rhs_ap, start=start, stop=stop),
                reads=reads, writes=writes)


def wview(w_ap, c0, c1):
    return w_ap[:, c0:c1].rearrange("(k p) n -> p k n", p=128)


class WStream:
    def __init__(self, s, name, kchunks, ncols, nslots):
        self.s = s
        self.slots = [s.sbuf(f"{name}_{i}", [128, kchunks, ncols], BF16) for i in range(nslots)]
        self.i = 0
        self.kchunks = kchunks

    def load(self, w_ap, c0, c1, kc=None):
        t = self.slots[self.i % len(self.slots)]
        self.i += 1
        kc = kc or self.kchunks
        n = c1 - c0
        self.s.dma("pool", t.t[:, 0:kc, 0:n], wview(w_ap, c0, c1), reads=[], writes=[t.b], slot=t.b)
        return t


def phase_mod(s, C, io, L, tabs):
    nc = s.nc
    if True:
        ccT = s.sbuf(f"ccT{L}", [128, 16, 2], F32)
        scT = s.sbuf(f"scT{L}", [128, 16, 2], BF16)
        s.dma("sp", ccT.t[:], io["ccT"][:, :, :], writes=[ccT.b], slot=ccT.b)
        s.op("act", lambda e: e.activation(scT.t[:], ccT.t[:], AF.Silu), reads=[ccT.b], writes=[scT.b])
        bmT = s.sbuf(f"bmT{L}", [128, 96], F32)
        s.dma("sp", bmT.t[:], io["b_modT"][L], writes=[bmT.b], slot=bmT.b)
        rows = [s.sbuf(f"modrow{L}_{i}", [2, 512], F32) for i in range(2)]
        ws = WStream(s, f"wmod{L}", 16, 512, 3)
        ps = [s.psum(f"pmod{L}_{i}", [128, 512]) for i in range(2)]
        pT = s.psum(f"pmodT{L}", [128, 512])
        wm = io["w_mod"][L]
        for j in range(24):
            wt = ws.load(wm, j * 512, (j + 1) * 512)
            p = ps[j % 2]
            row = rows[j % 2]
            for k in range(16):
                mm(s, p.t[0:2, :], scT.t[:, k, :], wt.t[:, k, :], k == 0, k == 15,
                   reads=[scT.b, wt.b], writes=[p.b])
            s.op("dve", lambda e, p=p, row=row: e.tensor_copy(row.t[0:2, :], p.t[0:2, :]),
                 reads=[p.b], writes=[row.b])
            for cc in range(4):
                c = j * 4 + cc
                mm(s, pT.t[:, 2 * c:2 * c + 2], row.t[0:2, cc * 128:(cc + 1) * 128], C.ident_f.t[0:2, 0:2],
                   True, True, reads=[row.b, C.ident_f.b], writes=[pT.b])
        modT = s.sbuf(f"modT{L}", [128, 96, 2], F32)
        s.op("dve", lambda e: e.tensor_tensor(
            out=modT.t[:], in0=pT.t[:, 0:192].rearrange("p (a b) -> p a b", b=2),
            in1=bmT.t[:].unsqueeze(2).to_broadcast([128, 96, 2]), op=ALU.add),
            reads=[pT.b, bmT.b], writes=[modT.b])
        g1 = s.sbuf(f"g1_{L}", [128, 16], F32)
        g2 = s.sbuf(f"g2_{L}", [128, 16], F32)
        s.dma("sp", g1.t[:], io["g_norm1T"][L], writes=[g1.b], slot=g1.b)
        s.dma("sp", g2.t[:], io["g_norm2T"][L], writes=[g2.b], slot=g2.b)
        mod = {}
        for who in range(2):
            tab = tabs[who]
            def mv(m, who=who):
                return modT.t[:, m * 16:(m + 1) * 16, who]
            s.op("dve", lambda e, mv=mv, tab=tab: e.scalar_tensor_tensor(
                out=tab.t[:, 0, :], in0=mv(1), scalar=1.0, in1=g1.t[:], op0=ALU.add, op1=ALU.mult),
                reads=[modT.b, g1.b], writes=[tab.b])
            s.op("dve", lambda e, mv=mv, tab=tab: e.tensor_copy(tab.t[:, 1, :], mv(0)),
                 reads=[modT.b], writes=[tab.b])
            s.op("dve", lambda e, mv=mv, tab=tab: e.tensor_copy(tab.t[:, 2, :], mv(2)),
                 reads=[modT.b], writes=[tab.b])
            s.op("dve", lambda e, mv=mv, tab=tab: e.scalar_tensor_tensor(
                out=tab.t[:, 3, :], in0=mv(4), scalar=1.0, in1=g2.t[:], op0=ALU.add, op1=ALU.mult),
                reads=[modT.b, g2.b], writes=[tab.b])
            s.op("dve", lambda e, mv=mv, tab=tab: e.tensor_copy(tab.t[:, 4, :], mv(3)),
                 reads=[modT.b], writes=[tab.b])
            s.op("dve", lambda e, mv=mv, tab=tab: e.tensor_copy(tab.t[:, 5, :], mv(5)),
                 reads=[modT.b], writes=[tab.b])
            mod[who] = tab
        return mod


def setup_consts(s, C, io):
    C.ident_f = s.sbuf("ident_f", [128, 128], F32, local=False)
    C.ident_b = s.sbuf("ident_b", [128, 128], BF16, local=False)
    C.ones_b = s.sbuf("ones_b", [128, 128], BF16, local=False)
    C.ones_f = s.sbuf("ones_f", [128, 128], F32, local=False)
    s.dma("sp", C.ident_f.t[:], io["ident"][:, :], writes=[C.ident_f.b], slot=C.ident_f.b)
    s.op("dve", lambda e: e.tensor_copy(C.ident_b.t[:], C.ident_f.t[:]), reads=[C.ident_f.b], writes=[C.ident_b.b])
    s.op("dve", lambda e: e.memset(C.ones_b.t[:], 1.0), writes=[C.ones_b.b])
    s.op("dve", lambda e: e.memset(C.ones_f.t[:], 1.0), writes=[C.ones_f.b])


def tok_blocks(n0, n1, bs=512):
    out = []
    t = n0
    while t < n1:
        out.append((t, min(t + bs, n1)))
        t += bs
    return out


def rms_rstd(s, C, xt, nt, ps_t, sq_t, rstd_t, nchunks=16, dim=D):
    s.op("act", lambda e: e.activation(sq_t.t[:, 0:nchunks, 0:nt], xt.t[:, 0:nchunks, 0:nt], AF.Square),
         reads=[xt.b], writes=[sq_t.b])
    for c in range(nchunks):
        mm(s, ps_t.t[:, 0:nt], C.ones_b.t[:, :], sq_t.t[:, c, 0:nt], c == 0, c == nchunks - 1,
           reads=[C.ones_b.b, sq_t.b], writes=[ps_t.b])
    s.op("dve", lambda e: e.tensor_scalar(out=rstd_t.t[:, 0:nt], in0=ps_t.t[:, 0:nt], scalar1=1.0 / dim,
                                          scalar2=EPS, op0=ALU.mult, op1=ALU.add),
         reads=[ps_t.b], writes=[rstd_t.b])
    s.op("act", lambda e: e.activation(rstd_t.t[:, 0:nt], rstd_t.t[:, 0:nt], AF.Sqrt),
         reads=[rstd_t.b], writes=[rstd_t.b])
    s.op("dve", lambda e: e.reciprocal(rstd_t.t[:, 0:nt], rstd_t.t[:, 0:nt]),
         reads=[rstd_t.b], writes=[rstd_t.b])


def phase_norm1(s, C, io, L, mod, hT):
    xs = [s.sbuf(f"n1x{i}", [128, 16, 512], F32) for i in range(2)]
    sqs = [s.sbuf(f"n1sq{i}", [128, 16, 512], BF16) for i in range(2)]
    rs = [s.sbuf(f"n1r{i}", [128, 512], F32) for i in range(2)]
    pss = [s.psum(f"n1p{i}", [128, 512]) for i in range(2)]
    blocks = [(0, 256, 1)] + [(a, b, 0) for a, b in tok_blocks(256, NTOK)]
    for bi, (t0, t1, who) in enumerate(blocks):
        nt = t1 - t0
        x, sq, r, ps = xs[bi % 2], sqs[bi % 2], rs[bi % 2], pss[bi % 2]
        if who == 1:
            src = io["ctxT"][:, t0:t1]
        else:
            src = io["xT"][:, t0 - 256:t1 - 256]
        s.dma("sp", x.t[:, :, 0:nt], src.rearrange("(c p) t -> p c t", p=128), writes=[x.b], slot=x.b)
        rms_rstd(s, C, x, nt, ps, sq, r)
        tab = mod[who]
        for c in range(16):
            s.op("dve", lambda e, x=x, r=r, c=c, nt=nt, tab=tab: e.scalar_tensor_tensor(
                out=x.t[:, c, 0:nt], in0=x.t[:, c, 0:nt], scalar=tab.t[:, 0, c:c + 1], in1=r.t[:, 0:nt],
                op0=ALU.mult, op1=ALU.mult), reads=[x.b, r.b, tab.b], writes=[x.b])
            s.op("act", lambda e, x=x, c=c, nt=nt, tab=tab, t0=t0, t1=t1: e.activation(
                hT.t[:, c, t0:t1], x.t[:, c, 0:nt], AF.Identity, bias=tab.t[:, 1, c:c + 1], scale=1.0),
                reads=[x.b, tab.b], writes=[hT.k(bi)])
    return [hT.k(bi) for bi in range(len(blocks))]


def phase_proj(s, C, io, L, hT, hbufs, pkv, pq, qkvT):
    w = io["w_in"][L]
    ws = WStream(s, "win", 16, 512, 3)
    pss = [s.psum(f"pp{i}", [128, 512]) for i in range(4)]
    stg = [s.sbuf(f"pstg{i}", [128, 512], F32) for i in range(4)]
    n = 0
    pieces = []
    for c0 in range(0, NKV, 512):
        pieces.append(("kv", c0, min(c0 + 512, NKV)))
    for c0 in range(0, NQC, 512):
        pieces.append(("q", c0, c0 + 512))
    for kind, c0, c1 in pieces:
        ncol = c1 - c0
        gofs = c0 if kind == "kv" else NKV + c0
        wt = ws.load(w, gofs, gofs + ncol)
        tiles = range(NTOK // 128) if kind == "kv" else range(NQ // 128)
        for ti in tiles:
            ps = pss[n % 4]
            sg = stg[n % 4]
            n += 1
            for k in range(16):
                mm(s, ps.t[:, 0:ncol], hT.t[:, k, ti * 128:(ti + 1) * 128], wt.t[:, k, 0:ncol], k == 0, k == 15,
                   reads=hbufs + [wt.b], writes=[ps.b])
            eng = "act" if n % 2 else "dve"
            if eng == "act":
                s.op("act", lambda e, ps=ps, sg=sg, ncol=ncol: e.copy(sg.t[:, 0:ncol], ps.t[:, 0:ncol]),
                     reads=[ps.b], writes=[sg.b])
            else:
                s.op("dve", lambda e, ps=ps, sg=sg, ncol=ncol: e.tensor_copy(sg.t[:, 0:ncol], ps.t[:, 0:ncol]),
                     reads=[ps.b], writes=[sg.b])
            dst = pkv if kind == "kv" else pq
            s.dma("sp", dst.t[ti * 128:(ti + 1) * 128, c0:c1], sg.t[:, 0:ncol], reads=[sg.b],
                  writes=[dst.k(ti)], slot=sg.b)
    gbase = NKV + NQC
    blocks = [(0, 256)] + tok_blocks(256, NTOK)
    for pi in range(3):
        wt = ws.load(w, gbase + pi * 512, gbase + (pi + 1) * 512)
        for cc in range(4):
            ch = pi * 4 + cc
            for (t0, t1) in blocks:
                nt = t1 - t0
                ps = pss[n % 4]
                sg = stg[n % 4]
                n += 1
                for k in range(16):
                    mm(s, ps.t[:, 0:nt], wt.t[:, k, cc * 128:(cc + 1) * 128], hT.t[:, k, t0:t1], k == 0, k == 15,
                       reads=hbufs + [wt.b], writes=[ps.b])
                if n % 2:
                    s.op("act", lambda e, ps=ps, sg=sg, nt=nt: e.copy(sg.t[:, 0:nt], ps.t[:, 0:nt]),
                         reads=[ps.b], writes=[sg.b])
                else:
                    s.op("dve", lambda e, ps=ps, sg=sg, nt=nt: e.tensor_copy(sg.t[:, 0:nt], ps.t[:, 0:nt]),
                         reads=[ps.b], writes=[sg.b])
                s.dma("sp", qkvT.t[ch * 128:(ch + 1) * 128, t0:t1], sg.t[:, 0:nt], reads=[sg.b],
                      writes=[qkvT.k(ch)], slot=sg.b)


IN_SPLITS = (512, 512, 64, 1024, 256, 256, 1536, 8, 8, 512, 6144)


def _col_perm(flip):
    offs = np.cumsum((0,) + IN_SPLITS)
    seg = lambda i: np.arange(offs[i], offs[i + 1])
    cq, ckv, kpe, gq, gk, gv, qkv, beta, dec, ogate, bgate = [seg(i) for i in range(11)]
    if flip:
        beta = np.concatenate([beta[4:], beta[:4]])
        dec = np.concatenate([dec[4:], dec[:4]])
    return np.concatenate([ckv, kpe, gk, gv, beta, dec, cq, gq, ogate, qkv, bgate])


def fmaj(v):
    return np.ascontiguousarray(np.swapaxes(v.reshape(v.shape[:-1] + (v.shape[-1] // 128, 128)), -1, -2))


_HOST_CACHE = {}


def host_shared(inputs, layers):
    sh = {}
    Ls = list(layers)
    sh["w_mod"] = np.ascontiguousarray(inputs["w_mod"][Ls])
    bm = inputs["b_mod"][Ls]
    sh["b_modT"] = np.ascontiguousarray(bm.reshape(len(Ls), 96, 128).transpose(0, 2, 1))
    sh["g_norm1T"] = fmaj(inputs["g_norm1"][Ls])
    sh["g_norm2T"] = fmaj(inputs["g_norm2"][Ls])
    sh["ident"] = np.eye(128, dtype=np.float32)
    w_in = inputs["w_in"][Ls]
    sh["w_in_par"] = [np.ascontiguousarray(w_in[:, :, _col_perm(f)]) for f in (0, 1)]
    kvb = inputs["w_mla_kvb"][Ls].reshape(len(Ls), 512, 4, 2, 128)
    sh["w_kvb"] = np.ascontiguousarray(kvb.transpose(0, 1, 3, 2, 4).reshape(len(Ls), 512, 1024))
    qb = inputs["w_mla_qb"][Ls].reshape(len(Ls), 512, 4, 192)
    sh["w_qb"] = np.ascontiguousarray(np.concatenate(
        [qb[..., :128].reshape(len(Ls), 512, 512), qb[..., 128:].reshape(len(Ls), 512, 256)], axis=-1))
    for k in ("g_mla_kv", "g_mla_q", "g_gqa_q", "g_gqa_k"):
        sh[k] = np.ascontiguousarray(inputs[k][Ls])
    sh["rope_par"] = [rope_tables(f) for f in (0, 1)]
    nl = len(Ls)
    bg = w_in[:, :, 4688:10832].reshape(nl, D, 3, 16, 128)
    sh["w_bg"] = np.ascontiguousarray(bg.transpose(0, 1, 3, 2, 4).reshape(nl, D, 16, 384))
    sh["w_up"] = np.ascontiguousarray(np.concatenate(
        [inputs["w_up_a"][Ls], inputs["w_up_b"][Ls], inputs["w_up_c"][Ls]], axis=1))
    for k in ("w_out", "w_ff1", "w_ff2", "g_gdn_out"):
        sh[k] = np.ascontiguousarray(inputs[k][Ls])
    sh["g_finalT"] = fmaj(inputs["g_final"])
    wcv = inputs["w_conv"][Ls]
    sh["w_convT_par"] = [np.ascontiguousarray((wcv[:, ::-1] if f else wcv).reshape(nl, 5, 12, 128).transpose(0, 3, 2, 1))
                         for f in (0, 1)]
    for k in ("a_log", "dt_bias"):
        v = inputs[k][Ls]
        sh[k + "_par"] = [np.ascontiguousarray((v[:, ::-1] if f else v).reshape(nl, 8)) for f in (0, 1)]
    idx = np.arange(128)
    same = (idx[:, None] // 64) == (idx[None, :] // 64)
    sh["mask_U"] = (same & (idx[:, None] <= idx[None, :])).astype(np.float32)
    sh["mask_L"] = np.ascontiguousarray(sh["mask_U"].T)
    sh["mask_SU"] = (same & (idx[:, None] < idx[None, :])).astype(np.float32)
    sh["mask_SL"] = np.ascontiguousarray(sh["mask_SU"].T)
    sh["mask_BD"] = same.astype(np.float32)
    return sh


def rope_tables(flip):
    t = np.arange(SEQ)
    if flip:
        t = t[::-1]
    row = (t // 64).astype(np.float32)
    col = (t % 64).astype(np.float32)
    out = {}
    for Dh in (64, 128):
        hf = Dh // 4
        inv = (np.float32(10000.0) ** (-np.arange(hf, dtype=np.float32) / np.float32(hf))).astype(np.float32)
        ar = row[:, None] * inv[None, :]
        ac = col[:, None] * inv[None, :]
        Cm = np.concatenate([np.cos(ar), np.cos(ar), np.cos(ac), np.cos(ac)], axis=1).astype(np.float32)
        Sm = np.concatenate([-np.sin(ar), np.sin(ar), -np.sin(ac), np.sin(ac)], axis=1).astype(np.float32)
        Cm = np.concatenate([np.ones((NCTX, Dh), np.float32), Cm], axis=0)
        Sm = np.concatenate([np.zeros((NCTX, Dh), np.float32), Sm], axis=0)
        out[f"rope_c{Dh}"] = np.ascontiguousarray(Cm)
        out[f"rope_s{Dh}"] = np.ascontiguousarray(Sm)
    return out


def host_core(inputs, sh, core, x_full=None, ctx_full=None):
    b, par = core // 2, core % 2
    x = inputs["x"] if x_full is None else x_full
    ctx = inputs["ctx"] if ctx_full is None else ctx_full
    xb = x[b]
    cb = ctx[b]
    if par:
        xb = xb[::-1]
        cb = cb[::-1]
    m = {}
    m["xT"] = np.ascontiguousarray(xb.T)
    m["ctxT"] = np.ascontiguousarray(cb.T)
    cc = np.stack([inputs["c"][b], inputs["c_ctx"]], axis=-1)
    m["ccT"] = np.ascontiguousarray(cc.reshape(16, 128, 2).transpose(1, 0, 2))
    for k in ("w_mod", "b_modT", "g_norm1T", "g_norm2T", "ident"):
        m[k] = sh[k]
    m["w_in"] = sh["w_in_par"][par]
    for k in ("w_kvb", "w_qb", "g_mla_kv", "g_mla_q", "g_gqa_q", "g_gqa_k", "w_bg", "w_up", "w_out", "w_ff1", "w_ff2",
              "g_gdn_out", "g_finalT", "mask_U", "mask_L", "mask_SU", "mask_SL", "mask_BD"):
        m[k] = sh[k]
    m.update(sh["rope_par"][par])
    m["w_convT"] = sh["w_convT_par"][par]
    m["a_log"] = sh["a_log_par"][par]
    m["dt_bias"] = sh["dt_bias_par"][par]
    return m


def declare_inputs(nc, nl, only=None):
    io = {}

    def inp(name, shape):
        if only is not None and name not in only:
            return
        io[name] = nc.dram_tensor(name, list(shape), F32, kind="ExternalInput").ap()
    inp("xT", [D, SEQ])
    inp("ctxT", [D, NCTX])
    inp("ccT", [128, 16, 2])
    inp("w_mod", [nl, D, 6 * D])
    inp("b_modT", [nl, 128, 96])
    inp("g_norm1T", [nl, 128, 16])
    inp("g_norm2T", [nl, 128, 16])
    inp("ident", [128, 128])
    inp("w_in", [nl, D, 10832])
    inp("w_kvb", [nl, 512, 1024])
    inp("w_qb", [nl, 512, 768])
    inp("g_mla_kv", [nl, 512])
    inp("g_mla_q", [nl, 512])
    inp("g_gqa_q", [nl, 128])
    inp("g_gqa_k", [nl, 128])
    for Dh in (64, 128):
        inp(f"rope_c{Dh}", [NTOK, Dh])
        inp(f"rope_s{Dh}", [NTOK, Dh])
    inp("w_bg", [nl, D, 16, 384])
    inp("w_up", [nl, D, D])
    inp("w_out", [nl, D, D])
    inp("w_ff1", [nl, D, DFF])
    inp("w_ff2", [nl, DFF, D])
    inp("g_gdn_out", [nl, 128])
    inp("g_finalT", [128, 16])
    inp("w_convT", [nl, 128, 12, 5])
    inp("a_log", [nl, 8])
    inp("dt_bias", [nl, 8])
    for nm in ("U", "L", "SU", "SL", "BD"):
        inp("mask_" + nm, [128, 128])
    return io


class Arena:
    def __init__(self, s, nbytes):
        self.s = s
        self.n = nbytes
        self.t = s.gstack.enter_context(s.nc.sbuf_tensor("arena", [128, nbytes // 2], BF16))
        self.free = [(0, nbytes)]
        self.pending = []
        s.hooks.append(self.apply_frees)
        s.arena = self

    def alloc(self, name, shape, dtype):
        esz = 4 if dtype == F32 else 2
        n = esz
        for d in shape:
            n *= d
        n = (n + 63) // 64 * 64
        for i, (o, sz) in enumerate(self.free):
            if sz >= n:
                if sz == n:
                    self.free.pop(i)
                else:
                    self.free[i] = (o + n, sz - n)
                ap = self.t[:, o // 2:(o + n) // 2]
                if dtype == F32:
                    ap = ap.bitcast(F32)
                nel = 1
                for d in shape:
                    nel *= d
                ap = ap[:, 0:nel]
                if len(shape) == 2:
                    ap = ap.rearrange("p (a b) -> p a b", b=shape[1])
                elif len(shape) == 3:
                    ap = ap.rearrange("p (a b c) -> p a b c", b=shape[1], c=shape[2])
                tl = Tile(ap, name)
                tl.region = (o, n)
                return tl
        raise RuntimeError(f"arena full allocating {name} {n}B; free={self.free}")

    def release(self, *tiles):
        for t in tiles:
            self.pending.append(t.region)

    def apply_frees(self):
        for r in self.pending:
            self.free.append(r)
        self.pending = []
        self.free.sort()
        merged = []
        for o, sz in self.free:
            if merged and merged[-1][0] + merged[-1][1] == o:
                merged[-1] = (merged[-1][0], merged[-1][1] + sz)
            else:
                merged.append((o, sz))
        self.free = merged


def transpose_bf(s, C, ps_ap, src_ap, k, reads, writes):
    return mm(s, ps_ap, src_ap, C.ident_b.t[0:k, 0:k], True, True, reads=list(reads) + [C.ident_b.b], writes=writes)


def row_rstd(s, ss, n, dim):
    pass


def emit_rstd(s, t, ap, dim):
    s.op("dve", lambda e: e.tensor_scalar(out=ap, in0=ap, scalar1=1.0 / dim, scalar2=EPS, op0=ALU.mult, op1=ALU.add),
         reads=[t.b], writes=[t.b])
    s.op("act", lambda e: e.activation(ap, ap, AF.Sqrt), reads=[t.b], writes=[t.b])
    s.op("dve", lambda e: e.reciprocal(ap, ap), reads=[t.b], writes=[t.b])


def emit_rope(s, x, xv, H, Dh, CH, SH, tmp, out_t, out_ap):
    hf = Dh // 4
    n = H * Dh
    def v4(ap):
        return ap.rearrange("p (a two h) -> p a two h", two=2, h=hf)
    tv = tmp.t[:, 0:n]
    s.op("dve", lambda e: e.tensor_tensor(out=v4(tv)[:, :, 0, :], in0=v4(xv)[:, :, 1, :], in1=v4(SH.t[:, 0:n])[:, :, 0, :], op=ALU.mult),
         reads=[x.b, SH.b], writes=[tmp.b])
    s.op("dve", lambda e: e.tensor_tensor(out=v4(tv)[:, :, 1, :], in0=v4(xv)[:, :, 0, :], in1=v4(SH.t[:, 0:n])[:, :, 1, :], op=ALU.mult),
         reads=[x.b, SH.b], writes=[tmp.b])
    s.op("dve", lambda e: e.tensor_tensor(out=xv, in0=xv, in1=CH.t[:, 0:n], op=ALU.mult),
         reads=[x.b, CH.b], writes=[x.b])
    s.op("dve", lambda e: e.tensor_tensor(out=out_ap, in0=xv, in1=tv, op=ALU.add),
         reads=[x.b, tmp.b], writes=[out_t.b])


def load_rope_tables(s, io, ti, tabs):
    t0 = ti * 128
    for nm, H in (("c64", 4), ("s64", 4), ("c128", 8), ("s128", 8)):
        Dh = 64 if "64" in nm else 128
        small = tabs[nm + "_s"]
        s.dma("sp", small.t[:, 0:Dh], io["rope_" + nm][t0:t0 + 128, :], writes=[small.b], slot=small.b)
        big = tabs[nm]
        s.op("pool", lambda e, big=big, small=small, H=H, Dh=Dh: e.tensor_copy(
            big.t[:, 0:H * Dh].rearrange("p (h d) -> p h d", d=Dh),
            small.t[:, 0:Dh].unsqueeze(1).to_broadcast([128, H, Dh])),
            reads=[small.b], writes=[big.b])


def alloc_rope_tabs(s, A, tag):
    tabs = {}
    for nm, w in (("c64", 256), ("s64", 256), ("c128", 1024), ("s128", 1024)):
        tabs[nm] = A.alloc(f"{tag}_{nm}", [w], F32)
        tabs[nm + "_s"] = A.alloc(f"{tag}_{nm}s", [128], F32)
    return tabs


def phase_kvpost(s, C, A, io, L, pkv, caches):
    KaT, KpeT, Va, KbT, Vb = caches
    wkvb = A.alloc("wkvb", [4, 1024], BF16)
    s.dma("pool", wkvb.t[:], io["w_kvb"][L].rearrange("(k p) n -> p k n", p=128), writes=[wkvb.b], slot=wkvb.b)
    gkv = A.alloc("gkv_bc", [512], F32)
    s.dma("sp", gkv.t[:], io["g_mla_kv"][L:L + 1, :].to_broadcast([128, 512]), writes=[gkv.b], slot=gkv.b)
    ggk = A.alloc("ggk_bc", [256], F32)
    for h in range(2):
        s.dma("sp", ggk.t[:, h * 128:(h + 1) * 128], io["g_gqa_k"][L:L + 1, :].to_broadcast([128, 128]),
              writes=[ggk.k(h)], slot=ggk.k(h))
    tabs = alloc_rope_tabs(s, A, "kvr")
    s.op("pool", lambda e: e.memset(Va.t[:, :, :, 128:129], 1.0), writes=[Va.b])
    s.op("pool", lambda e: e.memset(Vb.t[:, :, :, 128:129], 1.0), writes=[Vb.b])
    kvts = [A.alloc(f"kvt{i}", [NKV], F32) for i in range(2)]
    ckvn = [A.alloc(f"ckvn{i}", [512], BF16) for i in range(2)]
    ckvnT = [A.alloc(f"ckvnT{i}", [4, 128], BF16) for i in range(2)]
    tmp = A.alloc("kvtmp", [1024], F32)
    kbf = [A.alloc(f"kbf{i}", [320], BF16) for i in range(2)]
    st = [A.alloc(f"kvst{i}", [8], F32) for i in range(2)]
    pT = [s.psum(f"kvpT{i}", [128, 512]) for i in range(2)]
    pK = [s.psum(f"kvpK{i}", [128, 512]) for i in range(2)]
    pV = [s.psum(f"kvpV{i}", [128, 512]) for i in range(2)]
    pS = [s.psum(f"kvpS{i}", [128, 512]) for i in range(2)]
    import os
    PARTS = int(os.environ.get("KV_PARTS", "255"))
    for ti in range(int(os.environ.get("KV_TILES", NTOK // 128))):
        i = ti % 2
        kvt, cn, cnT, kb, stt = kvts[i], ckvn[i], ckvnT[i], kbf[i], st[i]
        s.dma("sp", kvt.t[:], pkv.t[ti * 128:(ti + 1) * 128, :], reads=[pkv.k(ti)], writes=[kvt.b], slot=kvt.b)
        load_rope_tables(s, io, ti, tabs)
        s.op("pool", lambda e, stt=stt: e.memset(stt.t[:], 0.0), writes=[stt.b])
        s.op("act", lambda e, kvt=kvt, stt=stt: e.activation(tmp.t[:, 0:512], kvt.t[:, 0:512], AF.Square,
                                                          accum_out=stt.t[:, 0:1]),
             reads=[kvt.b], writes=[tmp.b, stt.b])
        for h in range(2):
            s.op("act", lambda e, kvt=kvt, stt=stt, h=h: e.activation(
                tmp.t[:, 512 + h * 128:512 + (h + 1) * 128], kvt.t[:, 576 + h * 128:576 + (h + 1) * 128], AF.Square,
                accum_out=stt.t[:, 1 + h:2 + h]), reads=[kvt.b], writes=[tmp.b, stt.b])
        emit_rstd(s, stt, stt.t[:, 0:1], 512)
        emit_rstd(s, stt, stt.t[:, 1:3], 128)
        if PARTS & 1:
            s.op("dve", lambda e, kvt=kvt, stt=stt, cn=cn: e.scalar_tensor_tensor(
                out=cn.t[:], in0=kvt.t[:, 0:512], scalar=stt.t[:, 0:1], in1=gkv.t[:], op0=ALU.mult, op1=ALU.mult),
                reads=[kvt.b, stt.b, gkv.b], writes=[cn.b])
            p = pT[i]
            for c in range(4):
                transpose_bf(s, C, p.t[:, c * 128:(c + 1) * 128], cn.t[:, c * 128:(c + 1) * 128], 128, [cn.b], [p.b])
            s.op("act", lambda e, p=p, cnT=cnT: e.copy(cnT.t[:].rearrange("p a b -> p (a b)"), p.t[:, :]),
                 reads=[p.b], writes=[cnT.b])
            pk = pK[i]
            for h in range(4):
                for c in range(4):
                    mm(s, pk.t[:, h * 128:(h + 1) * 128], wkvb.t[:, c, h * 128:(h + 1) * 128], cnT.t[:, c, :],
                       c == 0, c == 3, reads=[wkvb.b, cnT.b], writes=[pk.b])
            s.op("dve", lambda e, pk=pk, ti=ti: e.tensor_copy(
                KaT.t[:, :, ti * 128:(ti + 1) * 128], pk.t[:, :].rearrange("p (h t) -> p h t", t=128)),
                reads=[pk.b], writes=[KaT.k(ti)])
            pv = pV[i]
            for c in range(4):
                mm(s, pv.t[:, :], cnT.t[:, c, :], wkvb.t[:, c, 512:1024], c == 0, c == 3,
                   reads=[wkvb.b, cnT.b], writes=[pv.b])
            s.op("act", lambda e, pv=pv, ti=ti: e.copy(
                Va.t[:, ti, :, 0:128], pv.t[:, :].rearrange("p (h d) -> p h d", d=128)),
                reads=[pv.b], writes=[Va.k(ti)])
        if PARTS & 2:
            emit_rope(s, kvt, kvt.t[:, 512:576], 1, 64, tabs["c64"], tabs["s64"], tmp, kb, kb.t[:, 256:320])
        if PARTS & 4:
            s.op("dve", lambda e, kvt=kvt, stt=stt: e.tensor_tensor(
                out=kvt.t[:, 576:832].rearrange("p (h d) -> p h d", d=128),
                in0=kvt.t[:, 576:832].rearrange("p (h d) -> p h d", d=128),
                in1=stt.t[:, 1:3].unsqueeze(2).to_broadcast([128, 2, 128]), op=ALU.mult),
                reads=[kvt.b, stt.b], writes=[kvt.b])
            s.op("dve", lambda e, kvt=kvt: e.tensor_tensor(out=kvt.t[:, 576:832], in0=kvt.t[:, 576:832], in1=ggk.t[:],
                                                           op=ALU.mult), reads=[kvt.b, ggk.k(0), ggk.k(1)], writes=[kvt.b])
            emit_rope(s, kvt, kvt.t[:, 576:832], 2, 128, tabs["c128"], tabs["s128"], tmp, kb, kb.t[:, 0:256])
        if (PARTS & 22) == 22:
            ps_ = pS[i]
            for h in range(2):
                transpose_bf(s, C, ps_.t[:, h * 128:(h + 1) * 128], kb.t[:, h * 128:(h + 1) * 128], 128, [kb.b], [ps_.b])
            if PARTS & 32:
                transpose_bf(s, C, ps_.t[0:64, 256:384], kb.t[:, 256:320], 128, [kb.b], [ps_.b])
            if PARTS & 64:
              s.op("dve", lambda e, ps_=ps_, ti=ti: e.tensor_copy(
                KbT.t[:, :, ti * 128:(ti + 1) * 128], ps_.t[:, 0:256].rearrange("p (h t) -> p h t", t=128)),
                reads=[ps_.b], writes=[KbT.k(ti)])
            if PARTS & 128:
              s.op("act", lambda e, ps_=ps_, ti=ti: e.copy(KpeT.t[0:64, ti * 128:(ti + 1) * 128], ps_.t[0:64, 256:384]),
                 reads=[ps_.b], writes=[KpeT.k(ti)])
        if PARTS & 8:
            s.op("act", lambda e, kvt=kvt, ti=ti: e.copy(
                Vb.t[:, ti, :, 0:128], kvt.t[:, 832:1088].rearrange("p (h d) -> p h d", d=128)),
                reads=[kvt.b], writes=[Vb.k(ti)])

    A.release(wkvb, gkv, ggk, tmp, *kvts, *ckvn, *ckvnT, *kbf, *st, *[v for v in tabs.values()])


MLA_SCALE = 192.0 ** -0.5
GQA_SCALE = 128.0 ** -0.5


def phase_attn(s, C, A, io, L, pq, caches, OT, do_ctx):
    KaT, KpeT, Va, KbT, Vb = caches
    wqb = A.alloc("wqb", [4, 768], BF16)
    s.dma("pool", wqb.t[:], io["w_qb"][L].rearrange("(k p) n -> p k n", p=128), writes=[wqb.b], slot=wqb.b)
    gq = A.alloc("gq_bc", [512], F32)
    s.dma("sp", gq.t[:], io["g_mla_q"][L:L + 1, :].to_broadcast([128, 512]), writes=[gq.b], slot=gq.b)
    ggq = A.alloc("ggq_bc", [1024], F32)
    for h in range(8):
        s.dma("sp", ggq.t[:, h * 128:(h + 1) * 128], io["g_gqa_q"][L:L + 1, :].to_broadcast([128, 128]),
              writes=[ggq.k(h)], slot=ggq.k(h))
    ggq_all = [ggq.k(h) for h in range(8)]
    tabs = alloc_rope_tabs(s, A, "qr")
    qts = [A.alloc(f"qt{i}", [1536], F32) for i in range(2)]
    cqn = [A.alloc(f"cqn{i}", [512], BF16) for i in range(2)]
    cqnT = [A.alloc(f"cqnT{i}", [4, 128], BF16) for i in range(2)]
    qpe = [A.alloc(f"qpe{i}", [256], F32) for i in range(2)]
    qbf = [A.alloc(f"qbf{i}", [1280], BF16) for i in range(2)]
    tmp = A.alloc("qtmp", [1024], F32)
    st = [A.alloc(f"qst{i}", [16], F32) for i in range(2)]
    QaT = A.alloc("QaT", [4, 512], BF16)
    QpeT = A.alloc("QpeT", [4, 512], BF16)
    QbT = A.alloc("QbT", [8, 512], BF16)
    PT = [A.alloc(f"PT{i}", [512], BF16) for i in range(3)]
    Otok = [A.alloc(f"Otok{i}", [1536], BF16) for i in range(4)]
    rinv = [A.alloc(f"rinv{i}", [4], F32) for i in range(4)]
    pS = [s.psum(f"atS{i}", [128, 512]) for i in range(2)]
    pO = [s.psum(f"atO{i}", [128, 512]) for i in range(4)]
    pX = [s.psum(f"atX{i}", [128, 512]) for i in range(2)]
    nx = [0]

    def nextX():
        nx[0] += 1
        return pX[nx[0] % 2]

    groups = []
    if do_ctx:
        groups.append(([0, 1], 2))
    groups.append(([2, 3, 4, 5], 18))
    groups.append(([6, 7, 8, 9], 18))
    npt = 0
    for qtiles, nkc in groups:
        nq = len(qtiles) * 128
        qbufs = []
        for j, qi in enumerate(qtiles):
            i = qi % 2
            qt, cn, cnT, qp, qb, stt = qts[i], cqn[i], cqnT[i], qpe[i], qbf[i], st[i]
            s.dma("sp", qt.t[:], pq.t[qi * 128:(qi + 1) * 128, 0:1536], reads=[pq.k(qi)], writes=[qt.b], slot=qt.b)
            load_rope_tables(s, io, qi, tabs)
            s.op("pool", lambda e, stt=stt: e.memset(stt.t[:], 0.0), writes=[stt.b])
            s.op("act", lambda e, qt=qt, stt=stt: e.activation(tmp.t[:, 0:512], qt.t[:, 0:512], AF.Square,
                                                            accum_out=stt.t[:, 0:1]),
                 reads=[qt.b], writes=[tmp.b, stt.b])
            for h in range(8):
                s.op("act", lambda e, qt=qt, stt=stt, h=h: e.activation(
                    tmp.t[:, 0:128], qt.t[:, 512 + h * 128:512 + (h + 1) * 128], AF.Square,
                    accum_out=stt.t[:, 1 + h:2 + h]), reads=[qt.b], writes=[tmp.b, stt.b])
            emit_rstd(s, stt, stt.t[:, 0:1], 512)
            emit_rstd(s, stt, stt.t[:, 1:9], 128)
            s.op("dve", lambda e, qt=qt, stt=stt, cn=cn: e.scalar_tensor_tensor(
                out=cn.t[:], in0=qt.t[:, 0:512], scalar=stt.t[:, 0:1], in1=gq.t[:], op0=ALU.mult, op1=ALU.mult),
                reads=[qt.b, stt.b, gq.b], writes=[cn.b])
            p = nextX()
            for c in range(4):
                transpose_bf(s, C, p.t[:, c * 128:(c + 1) * 128], cn.t[:, c * 128:(c + 1) * 128], 128, [cn.b], [p.b])
            s.op("act", lambda e, p=p, cnT=cnT: e.copy(cnT.t[:].rearrange("p a b -> p (a b)"), p.t[:, :]),
                 reads=[p.b], writes=[cnT.b])
            p = nextX()
            for h in range(4):
                for c in range(4):
                    mm(s, p.t[:, h * 128:(h + 1) * 128], wqb.t[:, c, h * 128:(h + 1) * 128], cnT.t[:, c, :],
                       c == 0, c == 3, reads=[wqb.b, cnT.b], writes=[p.b])
            s.op("dve", lambda e, p=p, j=j: e.tensor_copy(
                QaT.t[:, :, j * 128:(j + 1) * 128], p.t[:, :].rearrange("p (h t) -> p h t", t=128)),
                reads=[p.b], writes=[QaT.k(j)])
            p = nextX()
            for c in range(4):
                mm(s, p.t[:, 0:256], cnT.t[:, c, :], wqb.t[:, c, 512:768], c == 0, c == 3,
                   reads=[wqb.b, cnT.b], writes=[p.b])
            s.op("act", lambda e, p=p, qp=qp: e.copy(qp.t[:], p.t[:, 0:256]), reads=[p.b], writes=[qp.b])
            emit_rope(s, qp, qp.t[:, 0:256], 4, 64, tabs["c64"], tabs["s64"], tmp, qb, qb.t[:, 1024:1280])
            p = nextX()
            for h in range(4):
                transpose_bf(s, C, p.t[0:64, h * 128:(h + 1) * 128], qb.t[:, 1024 + h * 64:1024 + (h + 1) * 64], 128,
                             [qb.b], [p.b])
            s.op("act", lambda e, p=p, j=j: e.copy(
                QpeT.t[0:64, :, j * 128:(j + 1) * 128], p.t[0:64, :].rearrange("p (h t) -> p h t", t=128)),
                reads=[p.b], writes=[QpeT.k(j)])
            s.op("dve", lambda e, qt=qt, stt=stt: e.tensor_tensor(
                out=qt.t[:, 512:1536].rearrange("p (h d) -> p h d", d=128),
                in0=qt.t[:, 512:1536].rearrange("p (h d) -> p h d", d=128),
                in1=stt.t[:, 1:9].unsqueeze(2).to_broadcast([128, 8, 128]), op=ALU.mult),
                reads=[qt.b, stt.b], writes=[qt.b])
            s.op("dve", lambda e, qt=qt: e.tensor_tensor(out=qt.t[:, 512:1536], in0=qt.t[:, 512:1536], in1=ggq.t[:],
                                                         op=ALU.mult), reads=[qt.b] + ggq_all, writes=[qt.b])
            emit_rope(s, qt, qt.t[:, 512:1536], 8, 128, tabs["c128"], tabs["s128"], tmp, qb, qb.t[:, 0:1024])
            for half in range(2):
                p = nextX()
                for hh in range(4):
                    h = half * 4 + hh
                    transpose_bf(s, C, p.t[:, hh * 128:(hh + 1) * 128], qb.t[:, h * 128:(h + 1) * 128], 128,
                                 [qb.b], [p.b])
                s.op("dve", lambda e, p=p, j=j, half=half: e.tensor_copy(
                    QbT.t[:, half * 4:(half + 1) * 4, j * 128:(j + 1) * 128],
                    p.t[:, :].rearrange("p (h t) -> p h t", t=128)),
                    reads=[p.b], writes=[QbT.k(j)])
            qbufs += [QaT.k(j), QpeT.k(j), QbT.k(j)]
        for hh in range(12):
            mla = hh < 4
            scale = MLA_SCALE if mla else GQA_SCALE
            for kc in range(nkc):
                ps = pS[npt % 2]
                pt = PT[npt % 3]
                npt += 1
                ks = slice(kc * 128, (kc + 1) * 128)
                if mla:
                    mm(s, ps.t[:, 0:nq], KaT.t[:, hh, ks], QaT.t[:, hh, 0:nq], True, False,
                       reads=[KaT.k(kc)] + qbufs, writes=[ps.b])
                    mm(s, ps.t[:, 0:nq], KpeT.t[0:64, ks], QpeT.t[0:64, hh, 0:nq], False, True,
                       reads=[KpeT.k(kc)] + qbufs, writes=[ps.b])
                else:
                    h = hh - 4
                    mm(s, ps.t[:, 0:nq], KbT.t[:, h // 4, ks], QbT.t[:, h, 0:nq], True, True,
                       reads=[KbT.k(kc)] + qbufs, writes=[ps.b])
                s.op("act", lambda e, ps=ps, pt=pt, nq=nq, scale=scale: e.activation(
                    pt.t[:, 0:nq], ps.t[:, 0:nq], AF.Exp, scale=scale), reads=[ps.b], writes=[pt.b])
                for j in range(len(qtiles)):
                    if mla:
                        vap, vb = Va.t[:, kc, hh, :], Va.k(kc)
                    else:
                        vap, vb = Vb.t[:, kc, (hh - 4) // 4, :], Vb.k(kc)
                    mm(s, pO[j].t[:, 0:129], pt.t[:, j * 128:(j + 1) * 128], vap, kc == 0, kc == nkc - 1,
                       reads=[pt.b, vb, Va.b, Vb.b], writes=[pO[j].b])
            for j in range(len(qtiles)):
                s.op("dve", lambda e, j=j: e.reciprocal(rinv[j].t[:, 0:1], pO[j].t[:, 128:129]),
                     reads=[pO[j].b], writes=[rinv[j].b])
                s.op("dve", lambda e, j=j, hh=hh: e.tensor_scalar(
                    out=Otok[j].t[:, hh * 128:(hh + 1) * 128], in0=pO[j].t[:, 0:128], scalar1=rinv[j].t[:, 0:1],
                    scalar2=None, op0=ALU.mult), reads=[pO[j].b, rinv[j].b], writes=[Otok[j].b])
        for j, qi in enumerate(qtiles):
            for q4 in range(3):
                p = nextX()
                for hh in range(4):
                    h = q4 * 4 + hh
                    transpose_bf(s, C, p.t[:, hh * 128:(hh + 1) * 128], Otok[j].t[:, h * 128:(h + 1) * 128], 128,
                                 [Otok[j].b], [p.b])
                eng = "act" if q4 % 2 else "dve"
                if eng == "act":
                    s.op("act", lambda e, p=p, q4=q4, qi=qi: e.copy(
                        OT.t[:, q4 * 4:(q4 + 1) * 4, qi * 128:(qi + 1) * 128],
                        p.t[:, :].rearrange("p (h t) -> p h t", t=128)), reads=[p.b], writes=[OT.k(qi)])
                else:
                    s.op("dve", lambda e, p=p, q4=q4, qi=qi: e.tensor_copy(
                        OT.t[:, q4 * 4:(q4 + 1) * 4, qi * 128:(qi + 1) * 128],
                        p.t[:, :].rearrange("p (h t) -> p h t", t=128)), reads=[p.b], writes=[OT.k(qi)])
    A.release(wqb, gq, ggq, tmp, QaT, QpeT, QbT, *qts, *cqn, *cqnT, *qpe, *qbf, *st, *PT, *Otok, *rinv,
              *[v for v in tabs.values()])


def q_blocks(do_ctx):
    bl = []
    if do_ctx:
        bl.append((0, 256, 1))
    bl.append((256, 768, 0))
    bl.append((768, 1280, 0))
    return bl


def phase_merge_a(s, C, A, io, L, hTd, OT, yT, do_ctx):
    hT = A.alloc("hTq", [16, NQ], BF16)
    s.dma("sp", hT.t[:].rearrange("p a b -> p (a b)"), hTd.t[:, :], reads=[hTd.b], writes=[hT.b], slot=hT.b)
    wbg = [A.alloc(f"wbg{i}", [16, 384], BF16) for i in range(2)]
    wup = [A.alloc(f"wup{i}", [16, 128], BF16) for i in range(2)]
    bg = [A.alloc(f"bg{i}", [512], BF16) for i in range(6)]
    tt = [A.alloc(f"mt{i}", [512], F32) for i in range(6)]
    pB = [s.psum(f"mB{i}", [128, 512]) for i in range(3)]
    pU = [s.psum(f"mU{i}", [128, 512]) for i in range(4)]
    blocks = q_blocks(do_ctx)
    nb = nu = nt = 0
    KR = {0: range(0, 4), 1: range(4, 12), 2: range(12, 16)}
    for c in range(16):
        wb = wbg[c % 2]
        wu = wup[c % 2]
        s.dma("pool", wb.t[:], io["w_bg"][L, :, c].rearrange("(k p) n -> p k n", p=128), writes=[wb.b], slot=wb.b)
        s.dma("pool", wu.t[:], io["w_up"][L, :, c * 128:(c + 1) * 128].rearrange("(k p) n -> p k n", p=128),
              writes=[wu.b], slot=wu.b)
        for (t0, t1, who) in blocks:
            n = t1 - t0
            prods = []
            for br in range(3):
                pb = pB[nb % 3]
                nb += 1
                for k in range(16):
                    mm(s, pb.t[:, 0:n], wb.t[:, k, br * 128:(br + 1) * 128], hT.t[:, k, t0:t1], k == 0, k == 15,
                       reads=[wb.b, hT.b], writes=[pb.b])
                g = bg[nt % 6]
                s.op("act", lambda e, g=g, pb=pb, n=n: e.activation(g.t[:, 0:n], pb.t[:, 0:n], AF.Sigmoid),
                     reads=[pb.b], writes=[g.b])
                pu = pU[nu % 4]
                nu += 1
                ks = list(KR[br])
                for i, k in enumerate(ks):
                    mm(s, pu.t[:, 0:n], wu.t[:, k, :], OT.t[:, k, t0:t1], i == 0, i == len(ks) - 1,
                       reads=[wu.b] + OT.all(), writes=[pu.b])
                t = tt[nt % 6]
                nt += 1
                s.op("dve", lambda e, t=t, pu=pu, g=g, n=n: e.tensor_tensor(out=t.t[:, 0:n], in0=pu.t[:, 0:n],
                                                                           in1=g.t[:, 0:n], op=ALU.mult),
                     reads=[pu.b, g.b], writes=[t.b])
                prods.append(t)
            a, b, cc = prods
            s.op("pool", lambda e, a=a, b=b, n=n: e.tensor_tensor(out=a.t[:, 0:n], in0=a.t[:, 0:n], in1=b.t[:, 0:n],
                                                                 op=ALU.add), reads=[a.b, b.b], writes=[a.b])
            s.op("pool", lambda e, a=a, cc=cc, n=n, c=c, t0=t0, t1=t1: e.tensor_tensor(
                out=yT.t[:, c, t0:t1], in0=a.t[:, 0:n], in1=cc.t[:, 0:n], op=ALU.add),
                reads=[a.b, cc.b], writes=[yT.k(c)])
    A.release(hT, *wbg, *wup, *bg, *tt)


def phase_merge_b(s, C, A, io, L, yT, XT, mod, do_ctx):
    wo = [A.alloc(f"wo{i}", [16, 128], BF16) for i in range(3)]
    pM = [s.psum(f"oM{i}", [128, 512]) for i in range(4)]
    blocks = q_blocks(do_ctx)
    n_ = 0
    ybufs = yT.all()
    for c in range(16):
        w = wo[c % 3]
        s.dma("pool", w.t[:], io["w_out"][L, :, c * 128:(c + 1) * 128].rearrange("(k p) n -> p k n", p=128),
              writes=[w.b], slot=w.b)
        for (t0, t1, who) in blocks:
            n = t1 - t0
            p = pM[n_ % 4]
            n_ += 1
            for k in range(16):
                mm(s, p.t[:, 0:n], w.t[:, k, :], yT.t[:, k, t0:t1], k == 0, k == 15, reads=[w.b] + ybufs, writes=[p.b])
            tab = mod[who]
            s.op("dve", lambda e, p=p, n=n, c=c, t0=t0, t1=t1, tab=tab: e.scalar_tensor_tensor(
                out=XT.t[:, c, t0:t1], in0=p.t[:, 0:n], scalar=tab.t[:, 2, c:c + 1], in1=XT.t[:, c, t0:t1],
                op0=ALU.mult, op1=ALU.add), reads=[p.b, tab.b, XT.k((c, t0))], writes=[XT.k((c, t0))])
    A.release(*wo)


def phase_norm2(s, C, A, io, L, XT, mod, do_ctx, h2):
    blocks = q_blocks(do_ctx)
    sq = A.alloc("f_sq", [16, 512], BF16)
    rs = A.alloc("f_rs", [512], F32)
    tmp = [A.alloc(f"f_tmp{i}", [512], F32) for i in range(2)]
    pN = [s.psum(f"fN{i}", [128, 512]) for i in range(1)]
    xall = XT.all()
    for (t0, t1, who) in blocks:
        n = t1 - t0
        tab = mod[who]
        s.op("act", lambda e, n=n, t0=t0, t1=t1: e.activation(sq.t[:, :, 0:n], XT.t[:, :, t0:t1], AF.Square),
             reads=xall, writes=[sq.b])
        ps = pN[0]
        for c in range(16):
            mm(s, ps.t[:, 0:n], C.ones_b.t[:, :], sq.t[:, c, 0:n], c == 0, c == 15, reads=[C.ones_b.b, sq.b], writes=[ps.b])
        s.op("dve", lambda e, ps=ps, n=n: e.tensor_scalar(out=rs.t[:, 0:n], in0=ps.t[:, 0:n], scalar1=1.0 / D,
                                                          scalar2=EPS, op0=ALU.mult, op1=ALU.add),
             reads=[ps.b], writes=[rs.b])
        s.op("act", lambda e, n=n: e.activation(rs.t[:, 0:n], rs.t[:, 0:n], AF.Sqrt), reads=[rs.b], writes=[rs.b])
        s.op("dve", lambda e, n=n: e.reciprocal(rs.t[:, 0:n], rs.t[:, 0:n]), reads=[rs.b], writes=[rs.b])
        for c in range(16):
            t = tmp[c % 2]
            s.op("dve", lambda e, t=t, c=c, n=n, t0=t0, t1=t1, tab=tab: e.scalar_tensor_tensor(
                out=t.t[:, 0:n], in0=XT.t[:, c, t0:t1], scalar=tab.t[:, 3, c:c + 1], in1=rs.t[:, 0:n],
                op0=ALU.mult, op1=ALU.mult), reads=xall + [rs.b, tab.b], writes=[t.b])
            s.op("act", lambda e, t=t, c=c, n=n, t0=t0, t1=t1, tab=tab: e.activation(
                h2.t[:, c, t0:t1], t.t[:, 0:n], AF.Identity, bias=tab.t[:, 4, c:c + 1], scale=1.0),
                reads=[t.b, tab.b], writes=[h2.b])
    A.release(sq, rs, *tmp)


def phase_ffn(s, C, A, io, L, XT, mod, do_ctx, h2):
    blocks = q_blocks(do_ctx)
    p1 = [s.psum(f"f1_{i}", [128, 512]) for i in range(3)]
    p2 = [s.psum(f"f2_{i}", [128, 512]) for i in range(4)]
    w1s = [A.alloc(f"w1_{i}", [16, 256], BF16) for i in range(3)]
    w2s = [A.alloc(f"w2_{i}", [4, 2048], BF16) for i in range(2)]
    aT = [A.alloc(f"aT{i}", [4, NQ], BF16) for i in range(2)]
    rr = [A.alloc(f"f_r{i}", [512], F32) for i in range(2)]
    n1 = n2 = nw1 = ne = 0
    for g in range(DFF // 512):
        a = aT[g % 2]
        for half in range(2):
            w1 = w1s[nw1 % 3]
            nw1 += 1
            c0 = g * 512 + half * 256
            s.dma("pool", w1.t[:], io["w_ff1"][L, :, c0:c0 + 256].rearrange("(k p) n -> p k n", p=128),
                  writes=[w1.b], slot=w1.b)
            for fcl in range(2):
                fc = half * 2 + fcl
                for (t0, t1, who) in blocks:
                    n = t1 - t0
                    p = p1[n1 % 3]
                    n1 += 1
                    for k in range(16):
                        mm(s, p.t[:, 0:n], w1.t[:, k, fcl * 128:(fcl + 1) * 128], h2.t[:, k, t0:t1], k == 0, k == 15,
                           reads=[w1.b, h2.b], writes=[p.b])
                    r = rr[n1 % 2]
                    s.op("act", lambda e, r=r, p=p, n=n: e.activation(r.t[:, 0:n], p.t[:, 0:n], AF.Relu),
                         reads=[p.b], writes=[r.b])
                    s.op("pool", lambda e, r=r, a=a, fc=fc, n=n, t0=t0, t1=t1: e.tensor_tensor(
                        out=a.t[:, fc, t0:t1], in0=r.t[:, 0:n], in1=r.t[:, 0:n], op=ALU.mult),
                        reads=[r.b], writes=[a.b])
        w2 = w2s[g % 2]
        s.dma("pool", w2.t[:], io["w_ff2"][L, g * 512:(g + 1) * 512, :].rearrange("(k p) n -> p k n", p=128),
              writes=[w2.b], slot=w2.b)
        for c in range(16):
            for (t0, t1, who) in blocks:
                n = t1 - t0
                p = p2[n2 % 4]
                n2 += 1
                for k in range(4):
                    mm(s, p.t[:, 0:n], w2.t[:, k, c * 128:(c + 1) * 128], a.t[:, k, t0:t1], k == 0, k == 3,
                       reads=[w2.b, a.b], writes=[p.b])
                tab = mod[who]
                s.op("dve", lambda e, p=p, n=n, c=c, t0=t0, t1=t1, tab=tab: e.scalar_tensor_tensor(
                    out=XT.t[:, c, t0:t1], in0=p.t[:, 0:n], scalar=tab.t[:, 5, c:c + 1], in1=XT.t[:, c, t0:t1],
                    op0=ALU.mult, op1=ALU.add), reads=[p.b, tab.b, XT.k((c, t0))], writes=[XT.k((c, t0))])
    A.release(h2, *w1s, *w2s, *aT, *rr)


def phase_final(s, C, A, io, XT, outT):
    sq = A.alloc("fin_sq", [16, 512], BF16)
    rs = A.alloc("fin_rs", [512], F32)
    gf = A.alloc("fin_g", [16], F32)
    s.dma("sp", gf.t[:], io["g_finalT"][:, :], writes=[gf.b], slot=gf.b)
    st = [A.alloc(f"fin_st{i}", [512], F32) for i in range(3)]
    ps = s.psum("finp", [128, 512])
    xall = XT.all()
    n_ = 0
    for (t0, t1) in ((256, 768), (768, 1280)):
        n = 512
        s.op("act", lambda e, t0=t0, t1=t1: e.activation(sq.t[:, :, 0:512], XT.t[:, :, t0:t1], AF.Square),
             reads=xall, writes=[sq.b])
        for c in range(16):
            mm(s, ps.t[:, 0:n], C.ones_b.t[:, :], sq.t[:, c, 0:n], c == 0, c == 15, reads=[C.ones_b.b, sq.b], writes=[ps.b])
        s.op("dve", lambda e: e.tensor_scalar(out=rs.t[:, 0:512], in0=ps.t[:, 0:512], scalar1=1.0 / D, scalar2=EPS,
                                              op0=ALU.mult, op1=ALU.add), reads=[ps.b], writes=[rs.b])
        s.op("act", lambda e: e.activation(rs.t[:, :], rs.t[:, :], AF.Sqrt), reads=[rs.b], writes=[rs.b])
        s.op("dve", lambda e: e.reciprocal(rs.t[:, :], rs.t[:, :]), reads=[rs.b], writes=[rs.b])
        for c in range(16):
            t = st[n_ % 3]
            n_ += 1
            s.op("dve", lambda e, t=t, c=c, t0=t0, t1=t1: e.scalar_tensor_tensor(
                out=t.t[:, :], in0=XT.t[:, c, t0:t1], scalar=gf.t[:, c:c + 1], in1=rs.t[:, :],
                op0=ALU.mult, op1=ALU.mult), reads=xall + [rs.b, gf.b], writes=[t.b])
            s.dma("sp", outT.t[c * 128:(c + 1) * 128, t0 - 256:t1 - 256], t.t[:, :], reads=[t.b],
                  writes=[outT.k((c, t0))], slot=t.b)
    A.release(sq, rs, gf, *st)


GDN_SCALE = 128.0 ** -0.5
DBG = None
import os as _os
GDN_DIRS = int(_os.environ.get("GDN_DIRS", "3"))


def phase_gdn_pre(s, C, A, io, L, qkvT, pkv):
    nc = s.nc
    NT = NTOK // 128
    kT = A.alloc("g_kT", [4, NTOK], BF16)
    qT = A.alloc("g_qT", [4, NQ], BF16)
    ktok = A.alloc("g_ktok", [NT, 4, 128], BF16)
    vtok = A.alloc("g_vtok", [NT, 4, 128], BF16)
    Oacc = A.alloc("g_Oacc", [10, 512], F32)
    beta = A.alloc("g_beta", [NT, 8], F32)
    gg = A.alloc("g_g", [NT, 8], F32)
    msk = {}
    for nm in ("U", "L", "SU", "SL", "BD"):
        msk[nm] = A.alloc("g_m" + nm, [128], F32)
        s.dma("sp", msk[nm].t[:], io["mask_" + nm][:, :], writes=[msk[nm].b], slot=msk[nm].b)
    wc = A.alloc("g_wc", [12, 5], F32)
    s.dma("sp", wc.t[:], io["w_convT"][L], writes=[wc.b], slot=wc.b)
    alog = A.alloc("g_alog", [8], F32)
    dtb = A.alloc("g_dtb", [8], F32)
    s.dma("sp", alog.t[:], io["a_log"][L:L + 1, :].to_broadcast([128, 8]), writes=[alog.b], slot=alog.b)
    s.dma("sp", dtb.t[:], io["dt_bias"][L:L + 1, :].to_broadcast([128, 8]), writes=[dtb.b], slot=dtb.b)
    s.op("act", lambda e: e.activation(alog.t[:], alog.t[:], AF.Exp), reads=[alog.b], writes=[alog.b])
    s.op("dve", lambda e: e.tensor_scalar(out=alog.t[:], in0=alog.t[:], scalar1=-1.0, scalar2=None, op0=ALU.mult),
         reads=[alog.b], writes=[alog.b])
    s.op("pool", lambda e: e.memset(Oacc.t[:], 0.0), writes=[Oacc.b])

    raw = [A.alloc(f"g_raw{i}", [2052], F32) for i in range(2)]
    acc = [A.alloc(f"g_acc{i}", [2048], F32) for i in range(2)]
    sqb = A.alloc("g_sqb", [512], BF16)
    rinv = A.alloc("g_rinv", [512], F32)
    vbf = A.alloc("g_vbf", [2048], BF16)
    pA = [s.psum(f"gA{i}", [128, 512]) for i in range(2)]
    pTr = [s.psum(f"gT{i}", [128, 512]) for i in range(2)]
    ntr = 0
    for ch in range(12):
        kind, h = ch // 4, ch % 4
        seqs = [(0, 256)] + ([(256, 1280)] if kind == 0 else [(256, NTOK)])
        for (u0, u1) in seqs:
            T = u1 - u0
            i = (ch + (u0 > 0)) % 2
            r, a = raw[i], acc[i]
            hi = min(u1 + 2, NTOK) if (kind == 0 and u0 > 0) else u1
            s.op("pool", lambda e, r=r: e.memset(r.t[:, 0:2], 0.0), writes=[r.b])
            if hi == u1:
                s.op("pool", lambda e, r=r, T=T: e.memset(r.t[:, 2 + T:4 + T], 0.0), writes=[r.b])
            s.dma("sp", r.t[:, 2:2 + (hi - u0)], qkvT.t[ch * 128:(ch + 1) * 128, u0:hi], reads=[qkvT.k(ch)],
                  writes=[r.b], slot=r.b)
            s.op("dve", lambda e, r=r, a=a, T=T, ch=ch: e.tensor_scalar(
                out=a.t[:, 0:T], in0=r.t[:, 0:T], scalar1=wc.t[:, ch, 0:1], scalar2=None, op0=ALU.mult),
                reads=[r.b, wc.b], writes=[a.b])
            for j in range(1, 5):
                s.op("dve", lambda e, r=r, a=a, T=T, ch=ch, j=j: e.scalar_tensor_tensor(
                    out=a.t[:, 0:T], in0=r.t[:, j:j + T], scalar=wc.t[:, ch, j:j + 1], in1=a.t[:, 0:T],
                    op0=ALU.mult, op1=ALU.add), reads=[r.b, wc.b, a.b], writes=[a.b])
            s.op("act", lambda e, a=a, T=T: e.activation(a.t[:, 0:T], a.t[:, 0:T], AF.Silu), reads=[a.b], writes=[a.b])
            if kind < 2:
                dst = qT if kind == 0 else kT
                for (b0, b1) in tok_blocks(0, T):
                    n = b1 - b0
                    p = pA[ntr % 2]
                    s.op("act", lambda e, a=a, b0=b0, b1=b1, n=n: e.activation(sqb.t[:, 0:n], a.t[:, b0:b1], AF.Square),
                         reads=[a.b], writes=[sqb.b])
                    mm(s, p.t[:, 0:n], C.ones_b.t[:, :], sqb.t[:, 0:n], True, True, reads=[C.ones_b.b, sqb.b], writes=[p.b])
                    s.op("dve", lambda e, p=p, n=n: e.tensor_scalar(out=rinv.t[:, 0:n], in0=p.t[:, 0:n], scalar1=EPS,
                                                                    scalar2=None, op0=ALU.add), reads=[p.b], writes=[rinv.b])
                    s.op("act", lambda e, n=n: e.activation(rinv.t[:, 0:n], rinv.t[:, 0:n], AF.Sqrt),
                         reads=[rinv.b], writes=[rinv.b])
                    s.op("dve", lambda e, n=n: e.reciprocal(rinv.t[:, 0:n], rinv.t[:, 0:n]), reads=[rinv.b], writes=[rinv.b])
                    s.op("dve", lambda e, a=a, dst=dst, h=h, u0=u0, b0=b0, b1=b1, n=n: e.tensor_tensor(
                        out=dst.t[:, h, u0 + b0:u0 + b1], in0=a.t[:, b0:b1], in1=rinv.t[:, 0:n], op=ALU.mult),
                        reads=[a.b, rinv.b], writes=[dst.b])
                    ntr += 1
                src_bf, src_t = dst, dst
            else:
                s.op("act", lambda e, a=a, T=T: e.copy(vbf.t[:, 0:T], a.t[:, 0:T]), reads=[a.b], writes=[vbf.b])
            if kind >= 1:
                tdst = ktok if kind == 1 else vtok
                for g0 in range(u0 // 128, u1 // 128, 4):
                    g1 = min(g0 + 4, u1 // 128)
                    p = pTr[ntr % 2]
                    ntr += 1
                    for ti in range(g0, g1):
                        if kind == 1:
                            src = kT.t[:, h, ti * 128:(ti + 1) * 128]
                            rb = kT.b
                        else:
                            src = vbf.t[:, ti * 128 - u0:(ti + 1) * 128 - u0]
                            rb = vbf.b
                        transpose_bf(s, C, p.t[:, (ti - g0) * 128:(ti - g0 + 1) * 128], src, 128, [rb], [p.b])
                    s.op("act", lambda e, p=p, g0=g0, g1=g1, h=h, tdst=tdst: e.copy(
                        tdst.t[:, g0:g1, h, :], p.t[:, 0:(g1 - g0) * 128].rearrange("p (t d) -> p t d", d=128)),
                        reads=[p.b], writes=[tdst.b])

    braw = A.alloc("g_braw", [NT, 16], F32)
    for ti in range(NT):
        s.dma("sp", braw.t[:, ti, :], pkv.t[ti * 128:(ti + 1) * 128, 1088:1104], reads=[pkv.k(ti)],
              writes=[braw.k(ti)], slot=braw.k(ti))
    ball = [braw.k(ti) for ti in range(NT)]
    s.op("act", lambda e: e.activation(beta.t[:], braw.t[:, :, 0:8], AF.Sigmoid), reads=ball, writes=[beta.b])
    xs = A.alloc("g_xs", [NT, 8], F32)
    ax = A.alloc("g_ax", [NT, 8], F32)
    s.op("dve", lambda e: e.tensor_tensor(out=xs.t[:], in0=braw.t[:, :, 8:16],
                                          in1=dtb.t[:].unsqueeze(1).to_broadcast([128, NT, 8]), op=ALU.add),
         reads=ball + [dtb.b], writes=[xs.b])
    s.op("dve", lambda e: e.tensor_scalar(out=ax.t[:], in0=xs.t[:], scalar1=-1.0, scalar2=None, op0=ALU.mult),
         reads=[xs.b], writes=[ax.b])
    s.op("dve", lambda e: e.tensor_tensor(out=ax.t[:], in0=ax.t[:], in1=xs.t[:], op=ALU.min),
         reads=[xs.b, ax.b], writes=[ax.b])
    s.op("act", lambda e: e.activation(ax.t[:], ax.t[:], AF.Exp), reads=[ax.b], writes=[ax.b])
    zz = A.alloc("g_zz", [NT, 8], F32)
    z2 = A.alloc("g_z2", [NT, 8], F32)
    s.op("dve", lambda e: e.tensor_scalar(out=zz.t[:], in0=ax.t[:], scalar1=2.0, scalar2=None, op0=ALU.add),
         reads=[ax.b], writes=[zz.b])
    s.op("dve", lambda e: e.reciprocal(zz.t[:], zz.t[:]), reads=[zz.b], writes=[zz.b])
    s.op("dve", lambda e: e.tensor_tensor(out=zz.t[:], in0=zz.t[:], in1=ax.t[:], op=ALU.mult),
         reads=[zz.b, ax.b], writes=[zz.b])
    s.op("dve", lambda e: e.tensor_tensor(out=z2.t[:], in0=zz.t[:], in1=zz.t[:], op=ALU.mult),
         reads=[zz.b], writes=[z2.b])
    s.op("dve", lambda e: e.tensor_scalar(out=ax.t[:], in0=z2.t[:], scalar1=1.0 / 9.0, scalar2=None, op0=ALU.mult),
         reads=[z2.b], writes=[ax.b])
    for cst in (1.0 / 7.0, 1.0 / 5.0, 1.0 / 3.0):
        s.op("dve", lambda e, cst=cst: e.scalar_tensor_tensor(out=ax.t[:], in0=ax.t[:], scalar=cst, in1=z2.t[:],
                                                              op0=ALU.add, op1=ALU.mult), reads=[ax.b, z2.b], writes=[ax.b])
    s.op("dve", lambda e: e.scalar_tensor_tensor(out=ax.t[:], in0=ax.t[:], scalar=1.0, in1=zz.t[:],
                                                 op0=ALU.add, op1=ALU.mult), reads=[ax.b, zz.b], writes=[ax.b])
    s.op("dve", lambda e: e.tensor_scalar(out=ax.t[:], in0=ax.t[:], scalar1=2.0, scalar2=None, op0=ALU.mult),
         reads=[ax.b], writes=[ax.b])
    s.op("dve", lambda e: e.tensor_scalar(out=xs.t[:], in0=xs.t[:], scalar1=0.0, scalar2=None, op0=ALU.max),
         reads=[xs.b], writes=[xs.b])
    s.op("dve", lambda e: e.tensor_tensor(out=xs.t[:], in0=xs.t[:], in1=ax.t[:], op=ALU.add),
         reads=[xs.b, ax.b], writes=[xs.b])
    s.op("dve", lambda e: e.tensor_tensor(out=gg.t[:], in0=xs.t[:],
                                          in1=alog.t[:].unsqueeze(1).to_broadcast([128, NT, 8]), op=ALU.mult),
         reads=[xs.b, alog.b], writes=[gg.b])
    A.release(*raw, *acc, sqb, rinv, vbf, braw, xs, ax, zz, z2)
    return dict(kT=kT, qT=qT, ktok=ktok, vtok=vtok, Oacc=Oacc, beta=beta, gg=gg, msk=msk, wc=wc, alog=alog, dtb=dtb)


def phase_gdn_scan(s, C, A, io, L, G, pq, OT, do_ctx):
    kT, qT, ktok, vtok, Oacc, beta, gg, msk = (G[k] for k in ("kT", "qT", "ktok", "vtok", "Oacc", "beta", "gg", "msk"))
    NT = NTOK // 128
    banks = [s.psum(f"gs{i}", [128, 512]) for i in range(8)]
    nb = [0]

    def pb():
        nb[0] += 1
        return banks[nb[0] % 8]

    S32 = [A.alloc(f"g_S32_{i}", [128], F32) for i in range(8)]
    Sbf = [A.alloc(f"g_Sbf_{i}", [128], BF16) for i in range(8)]
    for i in range(8):
        s.op("pool", lambda e, i=i: e.memset(S32[i].t[:], 0.0), writes=[S32[i].b])
    NSET = 3
    def mk(nm, dt, n=128):
        return [A.alloc(f"g_{nm}{i}", [n], dt) for i in range(NSET)]
    T_gbc, T_dLs, T_dU, T_egr = mk("gbc", F32), mk("dLs", F32), mk("dU", F32), mk("egr", F32)
    T_N, T_NT, T_RT = mk("N", F32), mk("NT", F32), mk("RT", F32)
    T_X = [mk("Xa", F32), mk("Xb", F32)]
    T_Y = [mk("Ya", F32), mk("Yb", F32)]
    T_vb, T_kbg, T_kd, T_wT, T_qk, T_qg, T_vn = (mk(n_, F32) for n_ in ("vb", "kbg", "kd", "wT", "qk", "qg", "vn"))
    T_RT32 = mk("RT32", F32)
    T_u = mk("u", F32)
    T_egl = mk("egl", F32, 2)
    sm_t = [A.alloc(f"g_sm{i}", [24], F32) for i in range(4)]
    nset = [0]
    nsm = [0]

    def cfg(d):
        if d == 0:
            return dict(cum=msk["U"], M1=msk["L"], M1s=msk["SL"], M2=msk["U"], lastcol=63, order=(0, 1))
        return dict(cum=msk["L"], M1=msk["U"], M1s=msk["SU"], M2=msk["L"], lastcol=0, order=(1, 0))

    def per_td(t, d):
        cf = cfg(d)
        sm = sm_t[nsm[0] % 4]
        nsm[0] += 1
        p = pb()
        gsl = gg.t[:, t, d * 4:(d + 1) * 4]
        bsl = beta.t[:, t, d * 4:(d + 1) * 4]
        mm(s, p.t[:, 0:4], cf["cum"].t[:, :], gsl, True, True, reads=[cf["cum"].b, gg.b], writes=[p.b])
        mm(s, p.t[:, 4:8], msk["BD"].t[:, :], gsl, True, True, reads=[msk["BD"].b, gg.b], writes=[p.b])
        s.op("dve", lambda e: e.tensor_copy(sm.t[:, 0:8], p.t[:, 0:8]), reads=[p.b], writes=[sm.b])
        s.op("act", lambda e: e.activation(sm.t[:, 8:12], sm.t[:, 0:4], AF.Exp), reads=[sm.b], writes=[sm.b])
        s.op("dve", lambda e: e.tensor_tensor(out=sm.t[:, 8:12], in0=sm.t[:, 8:12], in1=bsl, op=ALU.mult),
             reads=[sm.b, beta.b], writes=[sm.b])
        s.op("dve", lambda e: e.tensor_tensor(out=sm.t[:, 12:16], in0=sm.t[:, 4:8], in1=sm.t[:, 0:4], op=ALU.subtract),
             reads=[sm.b], writes=[sm.b])
        s.op("act", lambda e: e.activation(sm.t[:, 12:16], sm.t[:, 12:16], AF.Exp), reads=[sm.b], writes=[sm.b])
        s.op("dve", lambda e: e.tensor_scalar(out=sm.t[:, 16:20], in0=bsl, scalar1=-1.0, scalar2=None, op0=ALU.mult),
             reads=[beta.b], writes=[sm.b])
        return sm

    def evac_bf(dst, p, eng):
        if eng == "act":
            s.op("act", lambda e: e.copy(dst.t[:, :], p.t[:, 0:128]), reads=[p.b], writes=[dst.b])
        else:
            s.op("dve", lambda e: e.tensor_copy(dst.t[:, :], p.t[:, 0:128]), reads=[p.b], writes=[dst.b])

    def per_thd(t, h, d, sm, has_q):
        cf = cfg(d)
        hd = d * 4 + h
        i = nset[0] % NSET
        nset[0] += 1
        gbc, dLs, dU, egr, N, NTt, RT = T_gbc[i], T_dLs[i], T_dU[i], T_egr[i], T_N[i], T_NT[i], T_RT[i]
        vb, kbg, kd, wT, qk, qg, vn, u, egl = T_vb[i], T_kbg[i], T_kd[i], T_wT[i], T_qk[i], T_qg[i], T_vn[i], T_u[i], T_egl[i]
        RT32 = T_RT32[i]
        tsl = slice(t * 128, (t + 1) * 128)
        gcol = gg.t[:, t, hd:hd + 1]
        s.op("dve", lambda e: e.tensor_scalar(out=gbc.t[:, :], in0=C.ones_f.t[:, :], scalar1=gcol, scalar2=None,
                                              op0=ALU.mult), reads=[C.ones_f.b, gg.b], writes=[gbc.b])
        p1 = pb()
        mm(s, p1.t[:, 0:128], gbc.t[:, :], cf["cum"].t[:, :], True, True, reads=[gbc.b, cf["cum"].b], writes=[p1.b])
        s.op("dve", lambda e: e.scalar_tensor_tensor(out=dLs.t[:, :], in0=p1.t[:, 0:128], scalar=sm.t[:, h:h + 1],
                                                     in1=cf["M1"].t[:, :], op0=ALU.subtract, op1=ALU.mult),
             reads=[p1.b, sm.b, cf["M1"].b], writes=[dLs.b])
        s.op("act", lambda e: e.activation(dLs.t[:, :], dLs.t[:, :], AF.Exp, scale=-1.0), reads=[dLs.b], writes=[dLs.b])
        s.op("dve", lambda e: e.tensor_tensor(out=dLs.t[:, :], in0=dLs.t[:, :], in1=cf["M1s"].t[:, :], op=ALU.mult),
             reads=[dLs.b, cf["M1s"].b], writes=[dLs.b])
        if has_q:
            s.op("dve", lambda e: e.scalar_tensor_tensor(out=dU.t[:, :], in0=p1.t[:, 0:128], scalar=sm.t[:, h:h + 1],
                                                         in1=cf["M2"].t[:, :], op0=ALU.subtract, op1=ALU.mult),
                 reads=[p1.b, sm.b, cf["M2"].b], writes=[dU.b])
            s.op("act", lambda e: e.activation(dU.t[:, :], dU.t[:, :], AF.Exp), reads=[dU.b], writes=[dU.b])
            s.op("dve", lambda e: e.scalar_tensor_tensor(out=dU.t[:, :], in0=dU.t[:, :], scalar=GDN_SCALE,
                                                         in1=cf["M2"].t[:, :], op0=ALU.mult, op1=ALU.mult),
                 reads=[dU.b, cf["M2"].b], writes=[dU.b])
            s.op("act", lambda e: e.activation(egr.t[:, :], p1.t[:, 0:128], AF.Exp), reads=[p1.b], writes=[egr.b])
        lc = cf["lastcol"]
        s.op("act", lambda e: e.activation(
            egl.t[:, 0:2], p1.t[:, 0:128].rearrange("p (c j) -> p c j", j=64)[:, :, lc], AF.Exp),
            reads=[p1.b], writes=[egl.b])
        pg = pb()
        mm(s, pg.t[:, 0:128], kT.t[:, h, tsl], kT.t[:, h, tsl], True, True, reads=[kT.b], writes=[pg.b])
        s.op("dve", lambda e: e.scalar_tensor_tensor(out=N.t[:, :], in0=pg.t[:, 0:128], scalar=sm.t[:, 16 + h:17 + h],
                                                     in1=dLs.t[:, :], op0=ALU.mult, op1=ALU.mult),
             reads=[pg.b, sm.b, dLs.b], writes=[N.b])
        pt = pb()
        mm(s, pt.t[:, 0:128], N.t[:, :], C.ident_f.t[:, :], True, True, reads=[N.b, C.ident_f.b], writes=[pt.b])
        evac_bf(NTt, pt, "act")
        s.op("dve", lambda e: e.tensor_tensor(out=RT.t[:, :], in0=NTt.t[:, :], in1=C.ident_f.t[:, :], op=ALU.add),
             reads=[NTt.b, C.ident_f.b], writes=[RT.b])
        X, Y = N, NTt
        for l in range(1, 6):
            Xn = T_X[l % 2][i]
            px = pb()
            mm(s, px.t[:, 0:128], Y.t[:, :], X.t[:, :], True, True, reads=[X.b, Y.b], writes=[px.b])
            evac_bf(Xn, px, "act")
            if l < 5:
                Yn = T_Y[l % 2][i]
                py = pb()
                mm(s, py.t[:, 0:128], X.t[:, :], Y.t[:, :], True, True, reads=[X.b, Y.b], writes=[py.b])
                evac_bf(Yn, py, "dve")
            pp = pb()
            mm(s, pp.t[:, 0:128], Xn.t[:, :], RT.t[:, :], True, True, reads=[Xn.b, RT.b], writes=[pp.b])
            s.op("dve", lambda e, pp=pp: e.tensor_tensor(out=RT.t[:, :], in0=RT.t[:, :], in1=pp.t[:, 0:128], op=ALU.add),
                 reads=[RT.b, pp.b], writes=[RT.b])
            X = Xn
            if l < 5:
                Y = Yn
        s.op("pool", lambda e: e.tensor_scalar(out=vb.t[:, :], in0=vtok.t[:, t, h, :], scalar1=beta.t[:, t, hd:hd + 1],
                                               scalar2=None, op0=ALU.mult), reads=[vtok.b, beta.b], writes=[vb.b])
        s.op("pool", lambda e: e.tensor_scalar(out=kbg.t[:, :], in0=ktok.t[:, t, h, :], scalar1=sm.t[:, 8 + h:9 + h],
                                               scalar2=None, op0=ALU.mult), reads=[ktok.b, sm.b], writes=[kbg.b])
        s.op("pool", lambda e: e.tensor_scalar(out=kd.t[:, :], in0=ktok.t[:, t, h, :], scalar1=sm.t[:, 12 + h:13 + h],
                                               scalar2=None, op0=ALU.mult), reads=[ktok.b, sm.b], writes=[kd.b])
        s.op("act", lambda e: e.copy(RT32.t[:, :], RT.t[:, :]), reads=[RT.b], writes=[RT32.b])
        pu = pb()
        mm(s, pu.t[:, 0:128], RT32.t[:, :], vb.t[:, :], True, True, reads=[RT32.b, vb.b], writes=[pu.b])
        s.op("act", lambda e: e.copy(u.t[:, :], pu.t[:, 0:128]), reads=[pu.b], writes=[u.b])
        pw = pb()
        mm(s, pw.t[:, 0:128], kbg.t[:, :], RT32.t[:, :], True, True, reads=[RT32.b, kbg.b], writes=[pw.b])
        evac_bf(wT, pw, "dve")
        if has_q:
            qsl = qT.t[:, h, tsl]
            pq_ = pb()
            mm(s, pq_.t[:, 0:128], kT.t[:, h, tsl], qsl, True, True, reads=[kT.b, qT.b], writes=[pq_.b])
            s.op("dve", lambda e: e.tensor_tensor(out=qk.t[:, :], in0=pq_.t[:, 0:128], in1=dU.t[:, :], op=ALU.mult),
                 reads=[pq_.b, dU.b], writes=[qk.b])
            s.op("dve", lambda e: e.scalar_tensor_tensor(out=qg.t[:, :], in0=qsl, scalar=GDN_SCALE, in1=egr.t[:, :],
                                                         op0=ALU.mult, op1=ALU.mult), reads=[qT.b, egr.b], writes=[qg.b])
        for c in cf["order"]:
            rows = slice(c * 64, (c + 1) * 64)
            pv = pb()
            mm(s, pv.t[:, 0:128], wT.t[:, :], S32[hd].t[:, :], True, True, reads=[wT.b, S32[hd].b], writes=[pv.b])
            s.op("dve", lambda e, pv=pv, rows=rows: e.tensor_tensor(out=vn.t[rows, :], in0=u.t[rows, :],
                                                                    in1=pv.t[rows, 0:128], op=ALU.subtract),
                 reads=[u.b, pv.b], writes=[vn.b])
            if has_q:
                po = pb()
                mm(s, po.t[:, 0:128], qg.t[:, :], S32[hd].t[:, :], True, False, reads=[qg.b, S32[hd].b], writes=[po.b])
                mm(s, po.t[:, 0:128], qk.t[rows, :], vn.t[rows, :], False, True, reads=[qk.b, vn.b], writes=[po.b])
                s.op("dve", lambda e, po=po, rows=rows: e.tensor_tensor(
                    out=Oacc.t[rows, t, h * 128:(h + 1) * 128], in0=Oacc.t[rows, t, h * 128:(h + 1) * 128],
                    in1=po.t[rows, 0:128], op=ALU.add), reads=[po.b, Oacc.k((t, h))], writes=[Oacc.k((t, h))])
            pss = pb()
            mm(s, pss.t[:, 0:128], kd.t[rows, :], vn.t[rows, :], True, True, reads=[kd.b, vn.b], writes=[pss.b])
            s.op("dve", lambda e, pss=pss, c=c: e.scalar_tensor_tensor(
                out=S32[hd].t[:, :], in0=S32[hd].t[:, :], scalar=egl.t[:, c:c + 1], in1=pss.t[:, 0:128],
                op0=ALU.mult, op1=ALU.add), reads=[S32[hd].b, egl.b, pss.b], writes=[S32[hd].b])

    seqA = [0, 1] + list(range(2, 10))
    seqB = [1, 0] + list(range(NT - 1, 1, -1))
    q_ok = (lambda t: t < 10) if do_ctx else (lambda t: 2 <= t < 10)
    for n in range(len(seqB)):
        for d, seq in ((0, seqA), (1, seqB)):
            if n >= len(seq) or not (GDN_DIRS & (1 << d)):
                continue
            t = seq[n]
            sm = per_td(t, d)
            for h in range(4):
                per_thd(t, h, d, sm, q_ok(t))

    if DBG:
        s.dma("sp", DBG["dG"].t[:, 0:144], gg.t[:].rearrange("p a b -> p (a b)"), reads=[gg.b], writes=[DBG["dG"].k(0)], slot=gg.b)
        s.dma("sp", DBG["dG"].t[:, 144:288], beta.t[:].rearrange("p a b -> p (a b)"), reads=[beta.b], writes=[DBG["dG"].k(1)], slot=beta.b)
        for i in range(8):
            s.dma("sp", DBG["dS"].t[:, i * 128:(i + 1) * 128], S32[i].t[:, :], reads=[S32[i].b], writes=[DBG["dS"].k(i)], slot=S32[i].b)
        s.dma("sp", DBG["dO"].t[:, :], Oacc.t[:].rearrange("p a b -> p (a b)"), reads=Oacc.all(), writes=[DBG["dO"].b], slot=Oacc.b)
    gout = A.alloc("g_gout", [512], F32)
    for h in range(4):
        s.dma("sp", gout.t[:, h * 128:(h + 1) * 128], io["g_gdn_out"][L:L + 1, :].to_broadcast([128, 128]),
              writes=[gout.k(h)], slot=gout.k(h))
    gouts = [gout.k(h) for h in range(4)]
    og = [A.alloc(f"g_og{i}", [512], F32) for i in range(2)]
    ob = [A.alloc(f"g_ob{i}", [512], BF16) for i in range(2)]
    sq = A.alloc("g_osq", [512], F32)
    st = [A.alloc(f"g_ost{i}", [4], F32) for i in range(2)]
    tiles = range(0, 10) if do_ctx else range(2, 10)
    for t in tiles:
        i = t % 2
        o_, g_, b_, st_ = Oacc, og[i], ob[i], st[i]
        s.dma("sp", g_.t[:], pq.t[t * 128:(t + 1) * 128, 1536:2048], reads=[pq.k(t)], writes=[g_.b], slot=g_.b)
        s.op("act", lambda e, g_=g_: e.activation(g_.t[:], g_.t[:], AF.Silu), reads=[g_.b], writes=[g_.b])
        s.op("pool", lambda e, st_=st_: e.memset(st_.t[:], 0.0), writes=[st_.b])
        oks = [Oacc.k((t, h)) for h in range(4)]
        for h in range(4):
            s.op("act", lambda e, t=t, h=h, st_=st_: e.activation(
                sq.t[:, 0:128], Oacc.t[:, t, h * 128:(h + 1) * 128], AF.Square, accum_out=st_.t[:, h:h + 1]),
                reads=oks + [Oacc.b], writes=[sq.b, st_.b])
        emit_rstd(s, st_, st_.t[:, 0:4], 128)
        s.op("dve", lambda e, t=t, st_=st_: e.tensor_tensor(
            out=Oacc.t[:, t, :].rearrange("p (h d) -> p h d", d=128),
            in0=Oacc.t[:, t, :].rearrange("p (h d) -> p h d", d=128),
            in1=st_.t[:, 0:4].unsqueeze(2).to_broadcast([128, 4, 128]), op=ALU.mult),
            reads=oks + [st_.b], writes=oks)
        s.op("dve", lambda e, t=t: e.tensor_tensor(out=Oacc.t[:, t, :], in0=Oacc.t[:, t, :], in1=gout.t[:], op=ALU.mult),
             reads=oks + gouts, writes=oks)
        s.op("dve", lambda e, t=t, g_=g_, b_=b_: e.tensor_tensor(out=b_.t[:], in0=Oacc.t[:, t, :], in1=g_.t[:], op=ALU.mult),
             reads=oks + [g_.b], writes=[b_.b])
        p = pb()
        for h in range(4):
            transpose_bf(s, C, p.t[:, h * 128:(h + 1) * 128], b_.t[:, h * 128:(h + 1) * 128], 128, [b_.b], [p.b])
        s.op("act", lambda e, p=p, t=t: e.copy(OT.t[:, 12:16, t * 128:(t + 1) * 128],
                                              p.t[:, :].rearrange("p (h t) -> p h t", t=128)),
             reads=[p.b], writes=[OT.k(t)])
    A.release(*[G[k] for k in ("kT", "qT", "ktok", "vtok", "Oacc", "beta", "gg", "wc", "alog", "dtb")],
              *msk.values(), *S32, *Sbf, gout, *og, *ob, sq, *st, *sm_t,
              *T_gbc, *T_dLs, *T_dU, *T_egr, *T_N, *T_NT, *T_RT, *T_X[0], *T_X[1], *T_Y[0], *T_Y[1],
              *T_vb, *T_kbg, *T_kd, *T_wT, *T_qk, *T_qg, *T_vn, *T_u, *T_egl, *T_RT32)


def build_program(do_ctx, final, debug=False):
    nc = bass.Bass("TRN2", target_bir_lowering=False)
    io = declare_inputs(nc, 1)
    st = ExitStack()
    with st:
        s = Sched(nc, st)
        C = Ctx()
        pkv = s.dram("pkv", [NTOK, NKV], F32)
        pq = s.dram("pq", [NQ, NQC], F32)
        qkvT = s.dram("qkvT", [NG, NTOK], F32)
        hTd = s.dram("hTd", [128, 16 * NQ], BF16)
        if final:
            outT = s.dram("outT", [D, NOWN], F32, kind="ExternalOutput")
        else:
            xo = s.dram("xT_out", [D, NOWN], F32, kind="ExternalOutput")
            co = s.dram("ctxT_out", [D, NCTX], F32, kind="ExternalOutput")
        with s.phase():
            setup_consts(s, C, io)
        pm = [s.sbuf(f"pmod{who}", [128, 6, 16], F32, local=False) for who in range(2)]
        A = Arena(s, 200 * 1024)
        with s.phase():
            phase_mod(s, C, io, 0, pm)
        hT = A.alloc("hT", [16, NTOK], BF16)
        with s.phase():
            hb = phase_norm1(s, C, io, 0, pm, hT)
        with s.phase():
            phase_proj(s, C, io, 0, hT, hb, pkv, pq, qkvT)
            s.dma("sp", hTd.t[:, :].rearrange("p (a b) -> p a b", b=NQ), hT.t[:, :, 0:NQ], reads=hT.all(),
                  writes=[hTd.b], slot=hT.b)
            A.release(hT)
        caches = (A.alloc("KaT", [4, NTOK], BF16), A.alloc("KpeT", [NTOK], BF16), A.alloc("Va", [18, 4, 129], BF16),
                  A.alloc("KbT", [2, NTOK], BF16), A.alloc("Vb", [18, 2, 129], BF16))
        OT = A.alloc("OT", [16, NQ], BF16)
        with s.phase():
            phase_kvpost(s, C, A, io, 0, pkv, caches)
        with s.phase():
            phase_attn(s, C, A, io, 0, pq, caches, OT, do_ctx)
            A.release(*caches)
        with s.phase():
            G = phase_gdn_pre(s, C, A, io, 0, qkvT, pkv)
        with s.phase():
            phase_gdn_scan(s, C, A, io, 0, G, pq, OT, do_ctx)
        yT = A.alloc("yT", [16, NQ], BF16)
        with s.phase():
            phase_merge_a(s, C, A, io, 0, hTd, OT, yT, do_ctx)
            A.release(OT)
        XT = A.alloc("XT", [16, NQ], F32)
        with s.phase():
            s.dma("sp", XT.t[:, :, 256:NQ], io["xT"][:, 0:NOWN].rearrange("(c p) t -> p c t", p=128),
                  writes=[XT.b], slot=XT.b)
            if do_ctx:
                s.dma("sp", XT.t[:, :, 0:256], io["ctxT"][:, :].rearrange("(c p) t -> p c t", p=128),
                      writes=[XT.k("c")], slot=XT.k("c"))
        with s.phase():
            phase_merge_b(s, C, A, io, 0, yT, XT, pm, do_ctx)
            A.release(yT)
        h2 = A.alloc("h2T", [16, NQ], BF16)
        with s.phase():
            phase_norm2(s, C, A, io, 0, XT, pm, do_ctx, h2)
        with s.phase():
            phase_ffn(s, C, A, io, 0, XT, pm, do_ctx, h2)
        with s.phase():
            if final:
                phase_final(s, C, A, io, XT, outT)
            else:
                s.dma("sp", xo.t[:, :].rearrange("(c p) t -> p c t", p=128), XT.t[:, :, 256:NQ], reads=XT.all(),
                      writes=[xo.b], slot=XT.b)
                s.dma("sp", co.t[:, :].rearrange("(c p) t -> p c t", p=128), XT.t[:, :, 0:256], reads=XT.all(),
                      writes=[co.b], slot=XT.k("c"))
        stats = dict(n_ops=s.n_total, cnt=dict(s.cnt), n_dma_sems=s.n_dma_sems)
    return nc, stats


_PROGS = {}


def get_program(do_ctx, final):
    key = (do_ctx, final)
    if key not in _PROGS:
        _PROGS[key] = build_program(do_ctx, final)[0]
    return _PROGS[key]


def run_layer(inputs, L, x_full, ctx_full, final):
    sh = host_shared(inputs, [L])
    maps = [host_core(inputs, sh, c, x_full=x_full, ctx_full=ctx_full) for c in range(8)]
    nc = get_program(not final, final)
    res = run_bass_kernel_spmd(nc, maps, core_ids=list(range(8)))
    return res.results


def assemble(results, key, n):
    out = np.empty((4, 2 * n if key != "ctxT_out" else n, D), np.float32)
    for c in range(8):
        b, par = c // 2, c % 2
        blk = np.asarray(results[c][key]).T
        if key == "ctxT_out":
            if par == 0:
                out[b] = blk
            continue
        if par == 0:
            out[b, 0:n] = blk
        else:
            out[b, n:2 * n] = blk[::-1]
    return out


def kernel(**inputs):
    inputs = {k: np.asarray(v) for k, v in inputs.items()}
    x = inputs["x"]
    ctx = inputs["ctx"]
    r0 = run_layer(inputs, 0, x, ctx, final=False)
    x1 = assemble(r0, "xT_out", NOWN)
    c1 = assemble(r0, "ctxT_out", NCTX)
    r1 = run_layer(inputs, 1, x1, c1, final=True)
    out = assemble(r1, "outT", NOWN)
    return out
```
